# Optimizing a Trainium2 kernel written in Bass

```python
import math
import jax, jax.numpy as jnp
from jax import lax
import numpy as np

D_MODEL = 1024
BATCH = 8
SEQ = 4096
DEPTH = 1
DEC_BATCH = 8
DEC_SEQ = 16
PAST_LEN = 1024

CHUNK = 64
HEAD_DIM = 64
N_HEADS_A = D_MODEL // HEAD_DIM
WIDTH_A = N_HEADS_A * HEAD_DIM
DECAY_LORA = 64
ICLR_LORA = 64
WIDTH_B = D_MODEL
CONV_W = 3
PLE_DIM = 256
EPS = 1e-6
GN_EPS = 64e-5
DECAY_SCALE = math.exp(-0.5)
SHIFT_COLS = 3 * WIDTH_A + DECAY_LORA + ICLR_LORA + WIDTH_A
CONV_COLS = 4 * WIDTH_B
GATE_COLS = 2 * D_MODEL
IN_COLS = SHIFT_COLS + CONV_COLS + GATE_COLS

kernel_name = "rwkv7_shortconv_gated_merge_stream_step"


def rmsnorm(x, g):
    xf = x.astype(jnp.float32)
    y = xf * lax.rsqrt(jnp.mean(xf * xf, axis=-1, keepdims=True) + EPS)
    return (y * g.astype(jnp.float32)).astype(x.dtype)


def wkv7_scan(S0, r, w, kh, a, v, kt):
    def step(S, inp):
        r_t, w_t, kh_t, a_t, v_t, kt_t = inp
        sa = jnp.einsum('bhvk,bhk->bhv', S, kh_t)
        S = (S * w_t[:, :, None, :]
             - sa[..., None] * (a_t * kh_t)[:, :, None, :]
             + v_t[..., None] * kt_t[:, :, None, :])
        y_t = jnp.einsum('bhvk,bhk->bhv', S, r_t)
        return S, y_t
    xs = (jnp.moveaxis(r, 1, 0), jnp.moveaxis(w, 1, 0), jnp.moveaxis(kh, 1, 0),
          jnp.moveaxis(a, 1, 0), jnp.moveaxis(v, 1, 0), jnp.moveaxis(kt, 1, 0))
    S, ys = lax.scan(step, S0, xs)
    return jnp.moveaxis(ys, 0, 1), S


def hybrid_layer(x, p, wkv0, shift0, conv0, g_norm, w_in, mu_shift, w_decay0, w_decay2,
                 w_iclr0, w_iclr2, k_removal, k_replace, r_bonus, gn_w, gn_b, conv_w,
                 w_o_a, w_o_b, w_out, g_ple, w_ple_gate, w_ple):
    B_, T, _ = x.shape
    f32 = jnp.float32
    h = rmsnorm(x, g_norm)
    proj = h @ w_in
    pa = proj[..., :SHIFT_COLS]
    pb = proj[..., SHIFT_COLS:SHIFT_COLS + CONV_COLS]
    pg = proj[..., SHIFT_COLS + CONV_COLS:]

    pa_prev = jnp.concatenate([shift0[:, None, :].astype(pa.dtype), pa[:, :-1]], axis=1)
    pa_mix = pa + mu_shift * (pa_prev - pa)
    new_shift = pa[:, -1]
    r, k, v, w_lr, a_lr, z_a = jnp.split(
        pa_mix, [WIDTH_A, 2 * WIDTH_A, 3 * WIDTH_A, 3 * WIDTH_A + DECAY_LORA,
                 3 * WIDTH_A + DECAY_LORA + ICLR_LORA], axis=-1)
    r, k, v = r.astype(f32), k.astype(f32), v.astype(f32)
    d = w_decay0.astype(f32) + jnp.tanh(w_lr.astype(f32)) @ w_decay2.astype(f32)
    w = jnp.exp(-DECAY_SCALE * jax.nn.sigmoid(d))
    a = jax.nn.sigmoid(w_iclr0.astype(f32) + a_lr.astype(f32) @ w_iclr2.astype(f32))
    kappa = k * k_removal.astype(f32)
    kt = k * (1.0 + (a - 1.0) * k_replace.astype(f32))
    hs = (B_, T, N_HEADS_A, HEAD_DIM)
    r_h, w_h, a_h, v_h, kt_h = (t.reshape(hs) for t in (r, w, a, v, kt))
    kap_h = kappa.reshape(hs)
    kh_h = kap_h / jnp.maximum(jnp.linalg.norm(kap_h, axis=-1, keepdims=True), 1e-12)
    y, S = wkv7_scan(wkv0.astype(f32), r_h, w_h, kh_h, a_h, v_h, kt_h)
    mu = jnp.mean(y, axis=-1, keepdims=True)
    var = jnp.mean(jnp.square(y - mu), axis=-1, keepdims=True)
    yn = ((y - mu) * lax.rsqrt(var + GN_EPS)).reshape(B_, T, WIDTH_A)
    yn = yn * gn_w.astype(f32) + gn_b.astype(f32)
    bonus = jnp.sum(r_h * kt_h * r_bonus.astype(f32), axis=-1, keepdims=True) * v_h
    o_a = (yn + bonus.reshape(B_, T, WIDTH_A)).astype(x.dtype) * jax.nn.silu(z_a)
    out_a = o_a @ w_o_a

    gb, gc, xb, z_b = jnp.split(pb, 4, axis=-1)
    u = gc * xb
    u_pad = jnp.concatenate([conv0.astype(u.dtype), u], axis=1)
    cv = sum(conv_w[j] * u_pad[:, j:j + T] for j in range(CONV_W))
    new_conv = u_pad[:, -(CONV_W - 1):]
    out_b = (gb * cv * jax.nn.silu(z_b)) @ w_o_b

    g_a, g_b = jnp.split(jax.nn.sigmoid(pg), 2, axis=-1)
    x = x + (g_a * out_a + g_b * out_b) @ w_out

    x = x + jax.nn.sigmoid(rmsnorm(x, g_ple) @ w_ple_gate) * (p @ w_ple)
    return x, S.astype(x.dtype), new_shift, new_conv


def setup_inputs(seed: int = 0) -> dict:
    key = jax.random.key(seed)
    ks = jax.random.split(key, 32)
    nrm = lambda k, s, sc=1.0: jax.random.normal(k, s, jnp.float32) * sc
    L = DEPTH
    return {
        "x_prompt": nrm(ks[0], (BATCH, SEQ, D_MODEL)),
        "x_sample": nrm(ks[1], (DEC_BATCH, DEC_SEQ, D_MODEL)),
        "state_wkv": nrm(ks[2], (L, DEC_BATCH, N_HEADS_A, HEAD_DIM, HEAD_DIM), 0.5),
        "state_shift": nrm(ks[3], (L, DEC_BATCH, SHIFT_COLS)),
        "state_conv": nrm(ks[4], (L, DEC_BATCH, CONV_W - 1, WIDTH_B)),
        "p_prompt": nrm(ks[5], (L, BATCH, SEQ, PLE_DIM)),
        "p_sample": nrm(ks[6], (L, DEC_BATCH, DEC_SEQ, PLE_DIM)),
        "g_norm": 1.0 + nrm(ks[7], (L, D_MODEL), 0.02),
        "w_in": nrm(ks[8], (L, D_MODEL, IN_COLS), D_MODEL ** -0.5),
        "mu_shift": jax.random.uniform(ks[9], (L, SHIFT_COLS), jnp.float32),
        "w_decay0": jax.random.uniform(ks[10], (L, WIDTH_A), jnp.float32, -2.0, 2.0),
        "w_decay2": nrm(ks[11], (L, DECAY_LORA, WIDTH_A), 0.3 * DECAY_LORA ** -0.5),
        "w_iclr0": nrm(ks[12], (L, WIDTH_A), 0.1),
        "w_iclr2": nrm(ks[13], (L, ICLR_LORA, WIDTH_A), 0.3 * ICLR_LORA ** -0.5),
        "k_removal": 0.85 + nrm(ks[14], (L, WIDTH_A), 0.02),
        "k_replace": 1.0 + nrm(ks[15], (L, WIDTH_A), 0.02),
        "r_bonus": nrm(ks[16], (L, N_HEADS_A, HEAD_DIM), 0.1),
        "gn_w": 1.0 + nrm(ks[17], (L, WIDTH_A), 0.02),
        "gn_b": nrm(ks[18], (L, WIDTH_A), 0.01),
        "conv_w": nrm(ks[19], (L, CONV_W, WIDTH_B), CONV_W ** -0.5),
        "w_o_a": nrm(ks[20], (L, WIDTH_A, D_MODEL), WIDTH_A ** -0.5),
        "w_o_b": nrm(ks[21], (L, WIDTH_B, D_MODEL), WIDTH_B ** -0.5),
        "w_out": nrm(ks[22], (L, D_MODEL, D_MODEL), D_MODEL ** -0.5),
        "g_ple": 1.0 + nrm(ks[23], (L, D_MODEL), 0.02),
        "w_ple_gate": nrm(ks[24], (L, D_MODEL, D_MODEL), D_MODEL ** -0.5),
        "w_ple": nrm(ks[25], (L, PLE_DIM, D_MODEL), PLE_DIM ** -0.5),
        "g_final": 1.0 + nrm(ks[26], (D_MODEL,), 0.02),
    }


def reference(x_prompt, x_sample, state_wkv, state_shift, state_conv, p_prompt, p_sample,
              g_norm, w_in, mu_shift, w_decay0, w_decay2, w_iclr0, w_iclr2, k_removal,
              k_replace, r_bonus, gn_w, gn_b, conv_w, w_o_a, w_o_b, w_out, g_ple,
              w_ple_gate, w_ple, g_final):
    Bp = x_prompt.shape[0]
    xp, xs = x_prompt, x_sample
    wkv_p, shift_p, conv_p, wkv_s, shift_s, conv_s = [], [], [], [], [], []
    for i in range(DEPTH):
        params = (g_norm[i], w_in[i], mu_shift[i], w_decay0[i], w_decay2[i], w_iclr0[i],
                  w_iclr2[i], k_removal[i], k_replace[i], r_bonus[i], gn_w[i], gn_b[i],
                  conv_w[i], w_o_a[i], w_o_b[i], w_out[i], g_ple[i], w_ple_gate[i], w_ple[i])
        wkv0 = jnp.zeros((Bp, N_HEADS_A, HEAD_DIM, HEAD_DIM), jnp.float32)
        shift0 = jnp.zeros((Bp, SHIFT_COLS), xp.dtype)
        conv0 = jnp.zeros((Bp, CONV_W - 1, WIDTH_B), xp.dtype)
        xp, s1, s2, s3 = hybrid_layer(xp, p_prompt[i], wkv0, shift0, conv0, *params)
        wkv_p.append(s1); shift_p.append(s2); conv_p.append(s3)
        xs, t1, t2, t3 = hybrid_layer(xs, p_sample[i], state_wkv[i], state_shift[i],
                                      state_conv[i], *params)
        wkv_s.append(t1); shift_s.append(t2); conv_s.append(t3)
    y_prompt = rmsnorm(xp, g_final)
    y_sample = rmsnorm(xs, g_final)
    return (y_prompt, y_sample,
            jnp.stack(wkv_p), jnp.stack(shift_p), jnp.stack(conv_p),
            jnp.stack(wkv_s), jnp.stack(shift_s), jnp.stack(conv_s))
```

```python
import contextlib
import numpy as np
import ml_dtypes
import concourse.bass as bass
import concourse.mybir as mybir
from concourse.bass_utils import run_bass_kernel_spmd

F32 = mybir.dt.float32
BF16 = mybir.dt.bfloat16
AF = mybir.ActivationFunctionType
ALU = mybir.AluOpType
AX = mybir.AxisListType

D = 1024
TP = 4096
TS = 16
NH = 16
HD = 64
NFC = 8
PLE = 256
EPS = 1e-6
GN_EPS = 64e-5
DS = float(np.exp(-0.5))
SHIFT_COLS = 4224
NSH = 33
IN_COLS = 10368
TT_P = 512
NSLOT = 3
LOOKAHEAD = 1
SAMPLE_POS_DEFAULT = 8

OFF_R, OFF_K, OFF_V, OFF_LR, OFF_ZA = 0, 1024, 2048, 3072, 3200
OFF_GB, OFF_GC, OFF_XB, OFF_ZB = 4224, 5248, 6272, 7296
OFF_GA, OFF_GBm = 8320, 9344

G_LORA = 0
G_A = 1
G_B = 9
G_GA = 17
G_GB = 19
G_WOA = 21
G_WOB = 23
G_WOUT = 25
G_WPG = 27
G_WPLE = 29
NGRP = 31

V_GNORM = 0
V_MU = 8
V_WD0 = 41
V_WI0 = 49
V_KREM = 57
V_KREP = 65
V_GNW = 73
V_GNB = 81
V_CONV = 89
V_GPLE = 113
V_RBON = 121
NV = 129


class Prog:
    ENGS = ("pe", "act", "dve", "pool", "sp")
    LAT = 0.70

    def __init__(self):
        self.ops = []
        self.lastw = {}
        self.readers = {}
        self.fences = []
        self.last_tok = None
        self.disabled = False

    def op(self, eng, fn, reads=(), writes=(), dma=None, dur=None, xfer=0.0):
        if self.disabled:
            return
        bank_r = [k for k in reads if isinstance(k, tuple) and k[0] == "BANK"]
        if bank_r:
            reads = [k for k in reads if not (isinstance(k, tuple) and k[0] == "BANK")]
            writes = list(writes) + bank_r
        idx = len(self.ops)
        deps = {}
        for f in self.fences:
            deps[f] = True
        for k in reads:
            t = self.lastw.get(k)
            if t is not None:
                deps[t] = True
        for k in writes:
            t = self.lastw.get(k)
            if t is not None:
                deps[t] = True
            for t in self.readers.get(k, ()):
                if t not in deps:
                    deps[t] = False
        self.ops.append(dict(eng=eng, fn=fn, deps=deps, dma=dma, dur=(0.3 if dur is None else dur), xfer=xfer,
                             lab=(str(writes[0]) if len(writes) else "-")))
        self.last_tok = idx
        for k in writes:
            self.lastw[k] = idx
            self.readers[k] = []
        for k in reads:
            self.readers.setdefault(k, []).append(idx)

    def schedule(self):
        import heapq
        ops = self.ops
        n = len(ops)
        succ = [[] for _ in range(n)]
        indeg = [0] * n
        for i, o in enumerate(ops):
            indeg[i] = len(o["deps"])
            for d in o["deps"]:
                succ[d].append(i)
        prio = [0.0] * n
        for i in range(n - 1, -1, -1):
            m = 0.0
            for sidx in succ[i]:
                if prio[sidx] > m:
                    m = prio[sidx]
            prio[i] = ops[i]["dur"] + ops[i]["xfer"] + m
        finish = [0.0] * n
        dready = [0.0] * n
        pend = {e: [] for e in self.ENGS}
        avail = {e: [] for e in self.ENGS}
        efree = {e: 0.0 for e in self.ENGS}
        order = {e: [] for e in self.ENGS}
        for i in range(n):
            if indeg[i] == 0:
                heapq.heappush(pend[ops[i]["eng"]], (0.0, i))
        done = 0
        while done < n:
            best = None
            for e in self.ENGS:
                pe_, av = pend[e], avail[e]
                while pe_ and pe_[0][0] <= efree[e]:
                    _, i = heapq.heappop(pe_)
                    heapq.heappush(av, (-prio[i], i))
                if av:
                    cand = (efree[e], 0, e)
                elif pe_:
                    cand = (pe_[0][0], 1, e)
                else:
                    continue
                if best is None or cand < best:
                    best = cand
            t, kind, e = best
            if kind == 0:
                _, i = heapq.heappop(avail[e])
            else:
                _, i = heapq.heappop(pend[e])
            start = max(efree[e], dready[i])
            o = ops[i]
            o["start"] = start
            o["why"] = ("eng" if efree[e] >= dready[i] else "dep")
            efree[e] = start + o["dur"]
            finish[i] = start + o["dur"] + o["xfer"]
            order[e].append(i)
            done += 1
            for sidx in succ[i]:
                same = ops[sidx]["eng"] == e and o["dma"] is None
                r = finish[i] + (0.0 if same else self.LAT)
                if r > dready[sidx]:
                    dready[sidx] = r
                indeg[sidx] -= 1
                if indeg[sidx] == 0:
                    heapq.heappush(pend[ops[sidx]["eng"]], (dready[sidx], sidx))
        self.order = order
        self.makespan = max(finish) if n else 0.0
        self.tok = [None] * n
        self.dcnt = {}
        for e in self.ENGS:
            c = 0
            for i in order[e]:
                o = ops[i]
                if o["dma"] is None:
                    c += 1
                    self.tok[i] = ("e:" + e, c)
                else:
                    sk = "dma:" + o["dma"]
                    self.dcnt[sk] = self.dcnt.get(sk, 0) + 16
                    self.tok[i] = (sk, self.dcnt[sk])
        self.q = {e: [] for e in self.ENGS}
        for e in self.ENGS:
            seen = {}
            for i in order[e]:
                o = ops[i]
                waits = {}
                for d, need in o["deps"].items():
                    od = ops[d]
                    if od["eng"] == e and od["dma"] is None and not need and o["dma"] is None:
                        continue
                    sk, val = self.tok[d]
                    if seen.get(sk, 0) >= val:
                        continue
                    if waits.get(sk, 0) < val:
                        waits[sk] = val
                for sk, v in waits.items():
                    seen[sk] = v
                inc = (self.tok[i][0], 16 if o["dma"] is not None else 1)
                self.q[e].append((list(waits.items()), o["fn"], inc))

    def sem_keys(self):
        ks = ["e:" + e for e in self.ENGS]
        ks += sorted(self.dcnt.keys())
        return ks


class _Stop(Exception):
    pass


DBG_LIST = []


def build_program(n_prompt_tiles=TP // TT_P, stop=None, debug=False):
    del DBG_LIST[:]
    def ckpt(name):
        if stop is not None and name == stop:
            P.disabled = True
    nc = bass.Bass("TRN2", target_bir_lowering=False)
    P = Prog()
    es = contextlib.ExitStack()

    def dram(name, shape, dt, kind):
        return nc.dram_tensor(name, list(shape), dt, kind=kind).ap()

    xP = dram("x_p", [TP, D], F32, "ExternalInput")
    xS = dram("x_s", [TS, D], F32, "ExternalInput")
    pP = dram("p_p", [TP, PLE], F32, "ExternalInput")
    pS = dram("p_s", [TS, PLE], F32, "ExternalInput")
    wkv0 = dram("wkv0T", [128, NFC, HD], F32, "ExternalInput")
    sh0 = dram("shift0", [128, NSH], F32, "ExternalInput")
    cv0 = dram("conv0", [128, NFC, 2], F32, "ExternalInput")
    w_in = dram("w_in", [D, IN_COLS], F32, "ExternalInput")
    w_oa = dram("w_o_a", [D, D], F32, "ExternalInput")
    w_ob = dram("w_o_b", [D, D], F32, "ExternalInput")
    w_out = dram("w_out", [D, D], F32, "ExternalInput")
    w_pg = dram("w_ple_gate", [D, D], F32, "ExternalInput")
    w_ple = dram("w_ple", [PLE, D], F32, "ExternalInput")
    w_lora = dram("w_lora", [128, D], F32, "ExternalInput")
    vecs_d = dram("vecs", [128, NV], F32, "ExternalInput")
    gfin_d = dram("gfin_bc", [128, D], F32, "ExternalInput")
    cbf_d = dram("cbf", [128, 320], BF16, "ExternalInput")
    cf_d = dram("cf32", [128, 128 + 64 + 5 * 128 + 5 * 32 + TT_P + TS], F32, "ExternalInput")

    yP = dram("y_p", [TP, D], F32, "ExternalOutput")
    yS = dram("y_s", [TS, D], F32, "ExternalOutput")
    o_wkv = [dram("wkv_p", [128, NFC, HD], F32, "ExternalOutput"),
             dram("wkv_s", [128, NFC, HD], F32, "ExternalOutput")]
    o_sh = [dram("shift_p", [128, NSH], F32, "ExternalOutput"),
            dram("shift_s", [128, NSH], F32, "ExternalOutput")]
    o_cv = [dram("conv_p", [128, NFC, 2], F32, "ExternalOutput"),
            dram("conv_s", [128, NFC, 2], F32, "ExternalOutput")]
    wsc = dram("wsc", [NGRP, 128, 8, 512], BF16, "Internal")

    def sb(name, shape, dt=F32):
        return es.enter_context(nc.sbuf_tensor("sb_" + name, list(shape), dt))

    def psum(name, shape, dt=F32):
        return es.enter_context(nc.psum_tensor("ps_" + name, list(shape), dt))

    TT = TT_P
    vecs = sb("vecs", [128, NV])
    omm = sb("omm", [128, NSH])
    omkr = sb("omkr", [128, 8])
    gfin = sb("gfin", [128, D])
    cbf = sb("cbf", [128, 320], BF16)
    cf = sb("cf", [128, 128 + 64 + 5 * 128 + 5 * 32 + TT_P + TS])
    wlora = sb("wlora", [128, D], BF16)
    RB = sb("RB", [128, 8, 128], BF16)
    ident = cbf[:, 0:128]
    sident = cbf[:, 128:192]
    bones = cbf[:, 192:320]
    bmean = cf[:, 0:128]
    sidf = cf[:, 128:192]
    o = 192
    maskcat = {64: cf[:, o:o + 320], 16: cf[:, o + 640:o + 720]}
    maskA = {64: cf[:, o:o + 128], 16: cf[:, o + 640:o + 672]}
    maskB = {64: cf[:, o + 128:o + 256], 16: cf[:, o + 672:o + 704]}
    maskL = {64: cf[:, o + 256:o + 320], 16: cf[:, o + 704:o + 720]}
    o2 = o + 640 + 160
    segm = {TT_P: cf[:, o2:o2 + TT_P], TS: cf[:, o2 + TT_P:o2 + TT_P + TS]}

    xt = [sb("xt0", [128, TT // 128, D]), sb("xt1", [128, TT // 128, D])]
    xnb = sb("xnb", [128, D], BF16)
    sq_junk = xnb
    sh0b = sb("sh0b", [128, NSH])
    stat = sb("stat", [128, 32])
    hT = sb("hT", [128, 8, TT], BF16)
    pT = sb("pT", [128, 2, TT], BF16)
    wbuf = [sb("wbuf%d" % i, [128, 8, 512], BF16) for i in range(NSLOT)]
    oa = sb("oa", [128, 8, TT], BF16)
    ob = sb("ob", [128, 8, TT], BF16)
    lrb = sb("lrb", [128, TT], BF16)
    lmix = sb("lmix", [128, TT])
    Sf = [sb("Sf%d" % i, [128, NFC, HD]) for i in range(2)]
    Sb = [[sb("Sb%d_%d" % (i, j), [128, NFC, HD], BF16) for j in range(2)] for i in range(2)]
    carry = [sb("carry%d" % i, [128, NSH]) for i in range(2)]
    convc = [sb("convc%d" % i, [128, NFC, 2]) for i in range(2)]
    shout = [sb("shout%d" % i, [128, NSH]) for i in range(2)]
    Bm = [sb("Bm%d" % i, [128, TT]) for i in range(2)]
    t_r = sb("t_r", [128, TT]); t_k = sb("t_k", [128, TT]); t_v = sb("t_v", [128, TT])
    t_zss = [sb("t_zs%d" % i, [128, TT], BF16) for i in range(4)]
    t_zraw = None
    t_sg = sb("t_sg", [128, TT]); t_a = sb("t_a", [128, TT]); t_cs = sb("t_cs", [128, TT])
    t_wi = sb("t_wi", [128, TT]); t_winv = sb("t_winv", [128, TT]); t_wx = sb("t_wx", [128, TT])
    t_kap = sb("t_kap", [128, TT + 2]); t_ksq = sb("t_ksq", [128, TT], BF16)
    t_nrm = sb("t_nrm", [128, TT]); t_kh = sb("t_kh", [128, TT]); t_beta = sb("t_beta", [128, TT])
    t_t1 = sb("t_t1", [128, TT]); t_kt = sb("t_kt", [128, TT])
    ARs = [sb("AR%d" % i, [128, 2, TT], BF16) for i in range(3)]
    Bts = [sb("Bt%d" % i, [128, TT], BF16) for i in range(2)]
    Kts = [sb("Kt%d" % i, [128, TT], BF16) for i in range(2)]
    Vb1 = sb("Vb", [128, TT], BF16)
    Vbs = [Vb1, Vb1]
    t_rk = t_ksq
    t_bons = [sb("t_bon%d" % i, [128, TT], BF16) for i in range(4)]
    t_ys = [sb("t_y%d" % i, [128, TT]) for i in range(2)]
    g_a = lmix
    g_b = sb("g_b", [128, TT])
    pb = g_b[:, :].bitcast(BF16).rearrange("p (b d) -> p b d", d=PLE)
    ubuf = t_kap
    t_xbs, t_c1, t_zb, t_g1, t_sga, t_m2b, t_fin = t_sg, t_a, t_cs, t_winv, t_wx, t_kt, t_t1
    NCH = TT // 64
    scanbig = sb("scanbig", [128, 4608], BF16)
    TKVall = [scanbig[:, 2560:4608].rearrange("p (j k) -> p j k", k=64),
              sb("TKVall1", [128, NCH * 4, 64], BF16)]
    MMall = [scanbig[:, 0:2560].rearrange("p (c k) -> p c k", k=320),
             sb("MMall1", [128, NCH, 320], BF16)]
    m1 = scanbig[:, 0:4096].rearrange("p (c k) -> p c k", k=TT)
    ALIAS_K = [("MM", 0, c) for c in range(NCH)] + [("TKV", 0, 0), ("TKV", 0, 4)]
    Tall0 = sb("Tall0", [128, NCH, 64], BF16)
    Tall = [Tall0, Tall0]
    Ahat = [sb("Ahat%d" % i, [128, NCH, 64], BF16) for i in range(2)]
    Vhat = [sb("Vhat%d" % i, [128, NCH, 64], BF16) for i in range(2)]
    XVn = [sb("XVn%d" % i, [128, 4, 64], BF16) for i in range(2)]
    wcb = [sb("wcb%d" % i, [128, NCH]) for i in range(3)]
    PPg = [[sb("PPg%d_%d" % (g, i), [128, 4, 128], BF16) for i in range(2)] for g in range(2)]
    Tg = [[sb("Tg%d_%d" % (g, i), [128, 4, 64], BF16) for i in range(2)] for g in range(2)]
    Bt2s = [sb("Bt2_%d" % i, [128, TT], BF16) for i in range(2)]
    Kt2s = [sb("Kt2_%d" % i, [128, TT], BF16) for i in range(2)]
    s_UT = [sb("s_UT%d" % i, [128, 64], BF16) for i in range(2)]

    pm = [psum("pm%d" % i, [128, 512]) for i in range(3)]
    psc = [psum("psc%d" % i, [128, 512]) for i in range(4)]
    ptb = psum("ptb", [128, 1024], BF16)
    PTB_K = ("BANK", 7)

    def _n(ap):
        try:
            return int(ap.free_size())
        except Exception:
            return 256

    def _dur(eng, n):
        if eng == "act":
            return 1.25 * (0.22 + n / 1350.0)
        if eng == "dve":
            return 1.12 * (0.07 + n / 850.0)
        if eng == "pool":
            return 1.15 * (0.10 + n / 440.0)
        return 0.3

    def ACT(out, in_, func, reads, writes, bias=None, scale=None, accum=None):
        kw = {}
        if bias is not None: kw["bias"] = bias
        if scale is not None: kw["scale"] = scale
        if accum is not None: kw["accum_out"] = accum
        P.op("act", lambda e: e.activation(out=out, in_=in_, func=func, **kw), reads, writes, dur=_dur("act", _n(out)))

    def TTOP(eng, out, in0, in1, op, reads, writes):
        P.op(eng, lambda e: e.tensor_tensor(out, in0, in1, op), reads, writes, dur=_dur(eng, _n(out)))

    def TS_(eng, out, in0, s1, s2, op0, op1, reads, writes):
        if s2 is None:
            P.op(eng, lambda e: e.tensor_scalar(out, in0, s1, None, op0), reads, writes, dur=_dur(eng, _n(out)))
        else:
            P.op(eng, lambda e: e.tensor_scalar(out, in0, s1, s2, op0, op1), reads, writes, dur=_dur(eng, _n(out)))

    def STT(eng, out, in0, scalar, in1, op0, op1, reads, writes):
        P.op(eng, lambda e: e.scalar_tensor_tensor(out, in0, scalar, in1, op0, op1), reads, writes,
             dur=_dur(eng, _n(out)))

    def COPY(eng, out, in_, reads, writes):
        if eng == "act":
            P.op("act", lambda e: e.copy(out, in_), reads, writes, dur=_dur("act", _n(out)))
        else:
            P.op(eng, lambda e: e.tensor_copy(out, in_), reads, writes, dur=_dur(eng, _n(out)))

    def DMA(eng, out, in_, reads, writes, key):
        try:
            nb = float(out.nbytes())
        except Exception:
            nb = 65536.0
        P.op(eng, lambda e: e.dma_start(out=out, in_=in_), reads, writes, dma=key,
             dur=(1.0 if eng == "pool" else 0.08), xfer=2.0 + nb / 1.6e5)

    def MM(mms, reads, writes):
        def fn(e):
            ins = None
            for m in mms:
                kw = {}
                if m.get("tp") is not None: kw["tile_position"] = m["tp"]
                ins = e.matmul(m["out"], lhsT=m["lhsT"], rhs=m["rhs"], start=m["start"], stop=m["stop"], **kw)
            return ins
        d = 0.03
        for m in mms:
            d += max(0.032, _n(m["rhs"]) / 1800.0)
        P.op("pe", fn, reads, writes, dur=d)

    def TR(trs, reads, writes):
        def fn(e):
            ins = None
            for t in trs:
                kw = {}
                if t.get("tp") is not None: kw["tile_position"] = t["tp"]
                ins = e.transpose(t["out"], t["in_"], t["ident"], **kw)
            return ins
        P.op("pe", fn, reads, writes, dur=0.15 + 0.06 * len(trs))

    rot = {"pm": 0, "psc": 0}

    allb = pm + psc

    def next_pm():
        i = rot["pm"]; rot["pm"] = (i + 1) % 7
        return allb[i], ("BANK", i)

    def next_psc():
        return next_pm()

    def DBG(name, ap, keys, dt=F32):
        if not debug or getattr(P, "disabled", False):
            return
        shp = list(ap.shape)
        d = nc.dram_tensor("dbg_" + name, shp, dt, kind="ExternalOutput").ap()
        DBG_LIST.append("dbg_" + name)
        DMA("sp", d, ap, keys, [("dbg", name)], "dbg")

    epsb = sb("epsb", [128, 2])
    P.op("pool", lambda e: e.memset(epsb[:, 0:1], EPS), [], ["epsb0"])
    P.op("pool", lambda e: e.memset(epsb[:, 1:2], GN_EPS), [], ["epsb1"])
    EPS_AP = epsb[:, 0:1]
    GNEPS_AP = epsb[:, 1:2]
    DMA("sp", vecs[:], vecs_d[:], [], ["vecs"], "c0a")
    DMA("sp", gfin[:], gfin_d[:], [], ["gfin"], "c0b")
    DMA("sp", cbf[:], cbf_d[:], [], ["cbf"], "c0c")
    DMA("sp", cf[:], cf_d[:], [], ["cf"], "c0d")
    DMA("pool", wlora[:], w_lora[:], [], ["wlora"], "c1")
    TS_("dve", omm[:], vecs[:, V_MU:V_MU + NSH], -1.0, 1.0, ALU.mult, ALU.add, ["vecs"], ["omm"])
    TS_("dve", omkr[:], vecs[:, V_KREP:V_KREP + 8], -1.0, 1.0, ALU.mult, ALU.add, ["vecs"], ["omkr"])
    for fc in range(8):
        TS_("dve", RB[:, fc, :], bones, vecs[:, V_RBON + fc:V_RBON + fc + 1], None, ALU.mult, None,
            ["vecs", "cbf"], [("RB", fc)])
    P.op("pool", lambda e: e.memset(Sf[0][:], 0.0), [], [("Sf", 0)])
    P.op("pool", lambda e: e.memset(Sb[0][0][:], 0.0), [], [("Sb", 0, 0)])
    P.op("pool", lambda e: e.memset(carry[0][:], 0.0), [], [("carry", 0)])
    P.op("pool", lambda e: e.memset(convc[0][:], 0.0), [], [("convc", 0)])
    DMA("sp", Sf[1][:], wkv0[:], [], [("Sf", 1)], "c2a")
    DMA("sp", sh0b[:], sh0[:], [], ["sh0b"], "c2b")
    DMA("sp", convc[1][:], cv0[:], [], [("convc", 1)], "c2c")
    COPY("dve", Sb[1][0][:], Sf[1][:], [("Sf", 1)], [("Sb", 1, 0)])
    TTOP("dve", carry[1][:], sh0b[:], vecs[:, V_MU:V_MU + NSH], ALU.mult,
         ["sh0b", "vecs"], [("carry", 1)])
    for tl in (TKVall + MMall + [Tall0] + Ahat + Vhat + XVn + PPg[0] + PPg[1] + Tg[0] + Tg[1] + s_UT):
        P.op("pool", (lambda t: (lambda e: e.memset(t[:], 0.0)))(tl), [], ["pool_init"])
    P.op("pool", lambda e: e.memset(stat[:], 0.0), [], ["pool_init"])
    pool_init_tok = P.last_tok
    for i in range(3):
        P.op("dve", (lambda t: (lambda e: e.memset(t[:], 0.0)))(pm[i]), [], [("BANK", i)])
    for i in range(4):
        P.op("dve", (lambda t: (lambda e: e.memset(t[:], 0.0)))(psc[i]), [], [("BANK", 3 + i)])
    P.fences.append(pool_init_tok)
    P.fences.append(P.last_tok)

    ckpt("init")
    w_in_v = w_in.rearrange("(kc p) c -> p kc c", p=128)

    cvt_groups = []

    def cvt(g, dst_cols, src, key_extra=None):
        if not cvt_groups or cvt_groups[-1][0] != g:
            cvt_groups.append([g, []])
        rd = list(cvt_groups[-3][1]) if len(cvt_groups) >= 3 else []
        key = ("wsc", g, dst_cols[0])
        cvt_groups[-1][1].append(key)
        DMA("pool", wsc[g][:, 0:src.shape[1], dst_cols[0]:dst_cols[1]], src, rd, [key], "cv%d" % g)

    def cvt_cols(g, offs):
        for j, off in enumerate(offs):
            cvt(g, (j * 128, (j + 1) * 128), w_in_v[:, :, off:off + 128])

    def sq_view(w, g):
        return w.rearrange("(kc p) c -> p kc c", p=128)[:, :, g * 512:(g + 1) * 512]

    cvt(G_LORA, (0, 128), w_in_v[:, :, OFF_LR:OFF_LR + 128])
    for fc in range(8):
        cvt_cols(G_A + fc, [OFF_R + fc * 128, OFF_K + fc * 128, OFF_V + fc * 128, OFF_ZA + fc * 128])
    for fc in range(8):
        cvt_cols(G_B + fc, [OFF_GB + fc * 128, OFF_GC + fc * 128, OFF_XB + fc * 128, OFF_ZB + fc * 128])
    for g in range(2):
        cvt(G_WOA + g, (0, 512), sq_view(w_oa, g))
        cvt(G_GA + g, (0, 512), w_in_v[:, :, OFF_GA + g * 512:OFF_GA + (g + 1) * 512])
    for g in range(2):
        cvt(G_WOB + g, (0, 512), sq_view(w_ob, g))
        cvt(G_GB + g, (0, 512), w_in_v[:, :, OFF_GBm + g * 512:OFF_GBm + (g + 1) * 512])
    for g in range(2):
        cvt(G_WOUT + g, (0, 512), sq_view(w_out, g))
    for g in range(2):
        cvt(G_WPG + g, (0, 512), sq_view(w_pg, g))
        cvt(G_WPLE + g, (0, 512), w_ple.rearrange("(kc p) c -> p kc c", p=128)[:, :, g * 512:(g + 1) * 512])

    ckpt("cvt")
    wseq_tile = ([G_LORA] + [G_A + f for f in range(8)] + [G_B + f for f in range(8)]
                 + [G_WOA, G_GA, G_WOA + 1, G_GA + 1, G_WOB, G_GB, G_WOB + 1, G_GB + 1,
                    G_WOUT, G_WOUT + 1, G_WPG, G_WPLE, G_WPG + 1, G_WPLE + 1])
    ntiles = 1 + n_prompt_tiles
    wseq = wseq_tile * ntiles
    wstate = {"issued": 0, "used": 0}

    def wsc_keys(g):
        if G_A <= g < G_GA:
            return [("wsc", g, j * 128) for j in range(4)]
        return [("wsc", g, 0)]

    def issue_wload():
        i = wstate["issued"]
        if i >= len(wseq):
            return
        g = wseq[i]
        slot = i % NSLOT
        if g == G_LORA:
            DMA("sp", wbuf[slot][:, :, 0:128], wsc[g][:, :, 0:128], wsc_keys(g), [("wbuf", slot)], "w%d" % slot)
        elif g >= G_WPLE:
            DMA("sp", wbuf[slot][:, 0:2, :], wsc[g][:, 0:2, :], wsc_keys(g), [("wbuf", slot)], "w%d" % slot)
        else:
            DMA("sp", wbuf[slot][:], wsc[g][:], wsc_keys(g), [("wbuf", slot)], "w%d" % slot)
        wstate["issued"] = i + 1

    def next_w(expect):
        i = wstate["used"]
        assert wseq[i] == expect, (wseq[i], expect)
        while wstate["issued"] <= min(i + LOOKAHEAD, len(wseq) - 1):
            issue_wload()
        wstate["used"] = i + 1
        slot = i % NSLOT
        return wbuf[slot], ("wbuf", slot)

    def run_tile(seq, tix, x_d, p_d, y_d, tok0, TTt, C, last, phases=("head", "mid", "tail")):
        nblk = max(1, TTt // 128)
        bt = min(128, TTt)
        nch = TTt // C
        L = {64: 6, 16: 4}[C]
        xtb = xt[tix % 2]
        xk = ("xt", tix % 2)
        vm = lambda col: vecs[:, col:col + 1]
        seg_ap = segm[TTt][:, 0:TTt]
        sl = slice(0, TTt)
        m_keys = [("oa", f) for f in range(8)]

        if "head" in phases:
            DMA("sp", xtb[0:bt, 0:nblk, :], x_d[tok0:tok0 + TTt, :].rearrange("(b p) d -> p b d", p=bt),
                [("xt_store", tix % 2, tb_) for tb_ in range(4)], [xk], "x%d" % (tix % 2))
        if "mid" in phases:
            DMA("pool", pb[0:bt, 0:nblk, :], p_d[tok0:tok0 + TTt, :].rearrange("(b p) d -> p b d", p=bt),
                [], ["g_b"], "p")

        ckpt("s0a")

        NSETS = [(xnb, "xnb", 0), (t_sg[:, :].bitcast(BF16), "t_sg", 4),
                 (t_a[:, :].bitcast(BF16), "t_a", 8), (t_cs[:, :].bitcast(BF16), "t_cs", 12),
                 (t_winv[:, :].bitcast(BF16), "t_winv", 16), (t_wx[:, :].bitcast(BF16), "t_wx", 20),
                 (t_wi[:, :].bitcast(BF16), "t_wi", 24), (t_nrm[:, :].bitcast(BF16), "t_nrm", 28)]

        def par_t(*gens):
            gens = list(gens)
            while gens:
                for g_ in list(gens):
                    try:
                        next(g_)
                        yield
                    except StopIteration:
                        gens.remove(g_)

        def norm_stats(src_blk, rd_keys, si):
            jb, jk, c0 = NSETS[si]
            kss, ksd, krs = "ss%d" % si, "sd%d" % si, "rstd%d" % si
            ACT(jb[0:bt, :], src_blk, AF.Square, rd_keys, [jk])
            yield
            P.op("dve", lambda e: e.tensor_reduce(stat[0:bt, c0:c0 + 1], jb[0:bt, :], axis=AX.X, op=ALU.add),
                 [jk], [kss], dur=1.2)
            yield
            ACT(stat[0:bt, c0 + 1:c0 + 2], stat[0:bt, c0:c0 + 1], AF.Sqrt, [kss, "epsb0"], [ksd],
                bias=EPS_AP[0:bt, :], scale=1.0 / D)
            yield
            P.op("dve", lambda e: e.reciprocal(stat[0:bt, c0 + 2:c0 + 3], stat[0:bt, c0 + 1:c0 + 2]), [ksd], [krs])
            yield

        def rms_to_T(src_blk, gcol, dstT, dst_key_prefix, tb, rd_keys, coarse=False, si=0):
            jb, jk, c0 = NSETS[si]
            yield from norm_stats(src_blk, rd_keys, si)
            rstd, krs = stat[0:bt, c0 + 2:c0 + 3], "rstd%d" % si
            TS_("dve", jb[0:bt, :], src_blk, rstd, None, ALU.mult, None, rd_keys + [krs], [jk])
            yield
            TR([dict(out=ptb[:, kc * 128:kc * 128 + bt], in_=jb[0:bt, kc * 128:(kc + 1) * 128],
                     ident=ident[0:bt, 0:bt]) for kc in range(8)], [jk, "cbf"], [PTB_K])
            wkeys_ = [((dst_key_prefix, kc) if coarse else (dst_key_prefix, kc, tb)) for kc in range(8)]
            gbc = vecs[:, gcol:gcol + 8].unsqueeze(2).to_broadcast([128, 8, bt])
            TTOP("dve", dstT[:, :, tb * bt:(tb + 1) * bt],
                 ptb[:, 0:1024].rearrange("p (k t) -> p k t", k=8)[:, :, 0:bt], gbc, ALU.mult,
                 [PTB_K, "vecs"], wkeys_)
            yield

        if "head" in phases:
            yield from par_t(*[rms_to_T(xtb[0:bt, tb, :], V_GNORM, hT, "hT", tb, [xk], si=4 + tb)
                               for tb in range(nblk)])
        hT_keys = [("hT", kc, tb) for kc in range(8) for tb in range(nblk)]
        if "mid" not in phases and "tail" not in phases:
            return
        if seq == 1:
            DBG("hT", hT[:, :, 0:TTt], hT_keys, BF16)
            DBG("stat", stat[:, 0:4], ["ss", "sd", "rstd"])
            DBG("xnb", xnb[0:bt, :], ["xnb"], BF16)
        ckpt("s0d")
        for tb in (range(nblk) if "mid" in phases else ()):
            TR([dict(out=ptb[:, kc * 128:kc * 128 + bt], in_=pb[0:bt, tb, kc * 128:(kc + 1) * 128],
                     ident=ident[0:bt, 0:bt]) for kc in range(2)], ["g_b", "cbf"], [PTB_K])
            COPY("dve", pT[:, :, tb * bt:(tb + 1) * bt],
                 ptb[:, 0:256].rearrange("p (k t) -> p k t", k=2)[:, :, 0:bt], [PTB_K], [("pT", tb)])
        pT_keys = [("pT", tb) for tb in range(nblk)]

        def proj_fm(wt, wkey, col0, ncols=128, rhsT=hT, rkeys=None, nk=8):
            ps, pk = next_pm()
            MM([dict(out=ps[0:ncols, 0:TTt], lhsT=wt[:, kc, col0:col0 + ncols], rhs=rhsT[:, kc, 0:TTt],
                     start=(kc == 0), stop=(kc == nk - 1)) for kc in range(nk)],
               [wkey] + (rkeys if rkeys is not None else hT_keys), [pk])
            return ps, pk

        def shift_mix(ps, pk, idx, out_ap, out_key):
            bm = Bm[idx % 2]; bk = ("Bm", idx % 2)
            ck = ("carry", seq, idx)
            ACT(bm[:, 0:TTt], ps[:, 0:TTt], AF.Copy, [pk, "vecs"], [bk], scale=vm(V_MU + idx))
            STT("dve", out_ap[:, 1:TTt], ps[:, 1:TTt], omm[:, idx:idx + 1], bm[:, 0:TTt - 1], ALU.mult, ALU.add,
                [pk, bk, "omm"], [out_key])
            STT("dve", out_ap[:, 0:1], ps[:, 0:1], omm[:, idx:idx + 1], carry[seq][:, idx:idx + 1], ALU.mult, ALU.add,
                [pk, ck, ("carry", seq), "omm"], [out_key, (out_key, "c0")])
            COPY("pool", carry[seq][:, idx:idx + 1], bm[:, TTt - 1:TTt], [bk], [ck])
            if last:
                COPY("act", shout[seq][:, idx:idx + 1], ps[:, TTt - 1:TTt], [pk], [("shout", seq, idx)])

        if "mid" in phases:
            ckpt("s0")
            wt, wk = next_w(G_LORA)
            ps, pk = proj_fm(wt, wk, 0)
            shift_mix(ps, pk, 24, lmix, "lmix")
            ACT(lrb[0:64, 0:TTt], lmix[0:64, 0:TTt], AF.Tanh, ["lmix", ("lmix", "c0")], ["lrb0"])
            COPY("pool", lrb[64:128, 0:TTt], lmix[64:128, 0:TTt], ["lmix", ("lmix", "c0")], ["lrb1"])

            ckpt("s1")
            sl = slice(0, TTt)
            rk_ = lambda nm: [nm, (nm, "c0")]
            GS = min(4, nch)
            hrow = lambda h: slice(64 * h, 64 * h + 64)
            hC = lambda h: slice(64 * h, 64 * h + C)
            tp_ = lambda h: (64 * h, 64 * h)
            mcat = maskcat[C]

            def P_stage(fc):
                db = fc % 2
                a3 = fc % 3
                AR = ARs[a3]; t_zs = t_zss[fc % 4]; t_bon = t_bons[fc % 4]
                Bt, Kt, Vb, Bt2, Kt2 = Bts[db], Kts[db], Vbs[db], Bt2s[db], Kt2s[db]
                kBt, kKt, kVb, kBt2, kKt2 = ("Bt", db), ("Kt", db), "Vb", ("Bt2", db), ("Kt2", db)
                zk = "t_zs%d" % (fc % 4)
                wt, wk = next_w(G_A + fc)
                for j, (dst, nm) in enumerate(((t_r, "t_r"), (t_k, "t_k"), (t_v, "t_v"), (t_kh, "t_kh"))):
                    ps, pk = next_pm()
                    for half in range(2):
                        MM([dict(out=ps[:, 0:TTt], lhsT=wt[:, kc, j * 128:(j + 1) * 128], rhs=hT[:, kc, 0:TTt],
                                 start=(kc == 0), stop=(kc == 7)) for kc in range(4 * half, 4 * half + 4)],
                           [wk] + hT_keys, [pk])
                        if half == 0:
                            yield
                    shift_mix(ps, pk, (0, 8, 16, 25)[j] + fc, dst, nm)
                    yield
                ps, pk = next_pm()
                MM([dict(out=ps[:, sl], lhsT=wlora[0:64, fc * 128:(fc + 1) * 128], rhs=lrb[0:64, sl],
                         start=True, stop=True)], ["wlora", "lrb0"], [pk])
                ACT(t_sg[:, sl], ps[:, sl], AF.Sigmoid, [pk, "vecs"], ["t_sg"], bias=vm(V_WD0 + fc))
                yield
                ps, pk = next_pm()
                MM([dict(out=ps[:, sl], lhsT=wlora[64:128, fc * 128:(fc + 1) * 128], rhs=lrb[64:128, sl],
                         start=True, stop=True, tp=(64, 0))], ["wlora", "lrb1"], [pk])
                ACT(t_a[:, sl], ps[:, sl], AF.Sigmoid, [pk, "vecs"], ["t_a"], bias=vm(V_WI0 + fc))
                ACT(t_zs[:, sl], t_kh[:, sl], AF.Silu, rk_("t_kh"), [zk])
                P.op("dve", lambda e: e.tensor_tensor_scan(t_cs[:, sl], seg_ap, t_sg[:, sl], 0.0,
                                                            op0=ALU.mult, op1=ALU.add),
                     ["t_sg", "cf"], ["t_cs"], dur=0.6)
                yield
                ACT(t_wi[:, sl], t_cs[:, sl], AF.Exp, ["t_cs"], ["t_wi"], scale=-DS)
                ACT(t_winv[:, sl], t_cs[:, sl], AF.Exp, ["t_cs"], ["t_winv"], scale=DS)
                yield
                TTOP("dve", t_wx[:, sl], t_cs[:, sl], t_sg[:, sl], ALU.subtract, ["t_cs", "t_sg"], ["t_wx"])
                ACT(t_wx[:, sl], t_wx[:, sl], AF.Exp, ["t_wx"], ["t_wx"], scale=-DS)
                COPY("pool", wcb[a3][:, 0:nch], t_wi[:, sl].rearrange("p (c t) -> p c t", t=C)[:, :, C - 1],
                     ["t_wi"], [("wcb", a3)])
                yield
                ACT(t_kap[:, sl], t_k[:, sl], AF.Copy, rk_("t_k") + ["vecs"], ["t_kap"], scale=vm(V_KREM + fc))
                ACT(t_ksq[:, sl], t_kap[:, sl], AF.Square, ["t_kap"], ["t_ksq"])
                ps, pk = next_pm()
                MM([dict(out=ps[:, sl], lhsT=bones, rhs=t_ksq[:, sl], start=True, stop=True)], ["t_ksq", "cbf"], [pk])
                yield
                TS_("dve", t_nrm[:, sl], ps[:, sl], 1e-24, None, ALU.max, None, [pk], ["t_nrm"])
                ACT(t_nrm[:, sl], t_nrm[:, sl], AF.Ln, ["t_nrm"], ["t_nrm"])
                ACT(t_beta[:, sl], t_nrm[:, sl], AF.Exp, ["t_nrm"], ["t_beta"], scale=-0.5)
                yield
                TTOP("dve", t_kh[:, sl], t_kap[:, sl], t_beta[:, sl], ALU.mult, ["t_kap", "t_beta"], ["t_kh"])
                TTOP("dve", t_beta[:, sl], t_a[:, sl], t_kh[:, sl], ALU.mult, ["t_a", "t_kh"], ["t_beta"])
                ACT(t_t1[:, sl], t_a[:, sl], AF.Identity, ["t_a", "vecs", "omkr"], ["t_t1"],
                    scale=vm(V_KREP + fc), bias=omkr[:, fc:fc + 1])
                yield
                TTOP("pool", t_kt[:, sl], t_k[:, sl], t_t1[:, sl], ALU.mult, rk_("t_k") + ["t_t1"], ["t_kt"])
                TTOP("dve", AR[:, 0, sl], t_kh[:, sl], t_wx[:, sl], ALU.mult, ["t_kh", "t_wx"], [("At", a3)])
                TTOP("pool", AR[:, 1, sl], t_r[:, sl], t_wi[:, sl], ALU.mult, rk_("t_r") + ["t_wi"], [("Rt", a3)])
                yield
                TTOP("dve", Bt[:, sl], t_beta[:, sl], t_winv[:, sl], ALU.mult, ["t_beta", "t_winv"], [kBt])
                TTOP("pool", Kt[:, sl], t_kt[:, sl], t_winv[:, sl], ALU.mult, ["t_kt", "t_winv"], [kKt])
                COPY("act", Vb[:, sl], t_v[:, sl], rk_("t_v"), [kVb])
                yield
                wcbc = wcb[a3][:, 0:nch].unsqueeze(2).to_broadcast([128, nch, C])
                TTOP("dve", Bt2[:, sl].rearrange("p (c t) -> p c t", t=C), Bt[:, sl].rearrange("p (c t) -> p c t", t=C),
                     wcbc, ALU.mult, [kBt, ("wcb", a3)], [kBt2])
                TTOP("dve", Kt2[:, sl].rearrange("p (c t) -> p c t", t=C), Kt[:, sl].rearrange("p (c t) -> p c t", t=C),
                     wcbc, ALU.mult, [kKt, ("wcb", a3)], [kKt2])
                yield
                TTOP("pool", t_rk[:, sl], t_r[:, sl], t_kt[:, sl], ALU.mult, rk_("t_r") + ["t_kt"], ["t_ksq"])
                ps, pk = next_pm()
                MM([dict(out=ps[:, sl], lhsT=RB[:, fc, :], rhs=t_rk[:, sl], start=True, stop=True)],
                   ["t_ksq", ("RB", fc)], [pk])
                TTOP("dve", t_bon[:, sl], ps[:, sl], t_v[:, sl], ALU.mult, [pk] + rk_("t_v"), [("t_bon", fc % 4)])
                ACT(t_bon[:, sl], t_bon[:, sl], AF.Identity, [("t_bon", fc % 4), "vecs"], [("t_bon", fc % 4)], bias=vm(V_GNB + fc))
                yield

            def I_pre(fc, g0):
                db = fc % 2
                a3 = fc % 3
                gi = (g0 // GS) % 2
                AR = ARs[a3]
                Bt, Kt, Vb, Bt2, Kt2 = Bts[db], Kts[db], Vbs[db], Bt2s[db], Kt2s[db]
                kBt, kKt, kVb, kBt2, kKt2 = ("Bt", db), ("Kt", db), "Vb", ("Bt2", db), ("Kt2", db)
                TKV, MMa = TKVall[db], MMall[db]
                TR([dict(out=ptb[hC(h), (i * 4 + j) * 64:(i * 4 + j + 1) * 64],
                         in_=src[hrow(h), (g0 + i) * C:(g0 + i + 1) * C], ident=sident[hrow(h), 0:64], tp=tp_(h))
                    for i in range(GS) for j, src in enumerate((Bt2, Kt2, Vb, AR[:, 0, :])) for h in range(2)],
                   [kBt2, kKt2, kVb, ("At", a3), "cbf"], [PTB_K])
                COPY("act", TKV[:, g0 * 4:(g0 + GS) * 4, :],
                     ptb[:, 0:GS * 256].rearrange("p (j k) -> p j k", k=64), [PTB_K], [("TKV", db, g0)])
                yield
                for i in range(GS):
                    c = g0 + i
                    cs = slice(c * C, (c + 1) * C)
                    psM_, pkM_ = next_psc()
                    mms = []
                    for h in range(2):
                        mms.append(dict(out=psM_[hC(h), 0:2 * C], lhsT=Bt[hrow(h), cs], rhs=AR[hrow(h), :, cs],
                                        start=True, stop=True, tp=tp_(h)))
                    for h in range(2):
                        mms.append(dict(out=psM_[hC(h), 2 * C:4 * C], lhsT=Kt[hrow(h), cs], rhs=AR[hrow(h), :, cs],
                                        start=True, stop=True, tp=tp_(h)))
                    for h in range(2):
                        mms.append(dict(out=psM_[hC(h), 4 * C:5 * C], lhsT=AR[hrow(h), 0, cs], rhs=Bt[hrow(h), cs],
                                        start=True, stop=True, tp=tp_(h)))
                    MM(mms, [kBt, kKt, ("At", a3), ("Rt", a3)], [pkM_])
                    TTOP("dve", MMa[:, c, 0:5 * C], psM_[:, 0:5 * C], mcat, ALU.mult, [pkM_, "cf"], [("MM", db, c)])
                    TTOP("pool", Tg[gi][0][:, i, 0:C], MMa[:, c, 0:C], sident[:, 0:C], ALU.add, [("MM", db, c), "cbf"],
                         [("Tg", gi, 0, i)])
                    yield

            def I_inv(fc, g0):
                db = fc % 2
                gi = (g0 // GS) % 2
                MMa, Ta = MMall[db], Tall[db]
                mmk = [("MM", db, g0 + i) for i in range(GS)]
                PP = PPg[gi]; TT_ = Tg[gi]
                for j in range(1, L + 1):
                    need_sq = (j <= L - 1)
                    need_T = (j >= 2)
                    PPo = PP[(j - 1) % 2]; PPok = ("PPg", gi, (j - 1) % 2)
                    PPn = PP[j % 2]; PPnk = ("PPg", gi, j % 2)
                    To = TT_[j % 2]; Tok = [("Tg", gi, j % 2, i) for i in range(GS)]
                    def Pc_PTc(i):
                        c = g0 + i
                        if j == 1:
                            return MMa[:, c, 0:C], MMa[:, c, 4 * C:5 * C]
                        return PPo[:, i, C:2 * C], PPo[:, i, 0:C]
                    rkeys = (mmk if j == 1 else [PPok])
                    if need_sq:
                        ps_, pk_ = next_psc()
                        mms = []
                        for i in range(GS):
                            Pc, PTc = Pc_PTc(i)
                            last_sq = (j == L - 1)
                            for h in range(2):
                                mms.append(dict(out=ps_[hC(h), i * 2 * C:i * 2 * C + C], lhsT=Pc[hC(h), :],
                                                rhs=PTc[hC(h), :], start=True, stop=True, tp=tp_(h)))
                            if not last_sq:
                                for h in range(2):
                                    mms.append(dict(out=ps_[hC(h), i * 2 * C + C:(i + 1) * 2 * C], lhsT=PTc[hC(h), :],
                                                    rhs=Pc[hC(h), :], start=True, stop=True, tp=tp_(h)))
                        MM(mms, rkeys, [pk_])
                    if need_T:
                        ps3, pk3 = next_psc()
                        mms = []
                        for i in range(GS):
                            Pc, PTc = Pc_PTc(i)
                            for h in range(2):
                                mms.append(dict(out=ps3[hC(h), i * C:(i + 1) * C], lhsT=PTc[hC(h), :], rhs=To[hC(h), i, 0:C],
                                                start=True, stop=True, tp=tp_(h)))
                        MM(mms, rkeys + Tok, [pk3])
                    if need_sq:
                        COPY("act", PPn[:, 0:GS, 0:2 * C], ps_[:, 0:GS * 2 * C].rearrange("p (i k) -> p i k", k=2 * C),
                             [pk_], [PPnk])
                    if need_T:
                        if j == L:
                            dstT = Ta[:, g0:g0 + GS, 0:C]; dk = [("Tall", g0)]
                        else:
                            dstT = TT_[(j + 1) % 2][:, 0:GS, 0:C]; dk = [("Tg", gi, (j + 1) % 2, i) for i in range(GS)]
                        TTOP("dve", dstT, ps3[:, 0:GS * C].rearrange("p (i k) -> p i k", k=C), To[:, 0:GS, 0:C], ALU.add,
                             [pk3] + Tok, dk)
                    yield

            def I_post(fc, g0):
                db = fc % 2
                gi = (g0 // GS) % 2
                TKV, MMa, Ta = TKVall[db], MMall[db], Tall[db]
                tkvk = ("TKV", db, g0); tk_ = ("Tall", g0)
                mmk = [("MM", db, g0 + i) for i in range(GS)]
                ps1, pk1 = next_psc()
                MM([dict(out=ps1[hC(h), i * 64:(i + 1) * 64], lhsT=MMa[hC(h), g0 + i, 2 * C:3 * C],
                         rhs=TKV[hC(h), (g0 + i) * 4 + 2, :], start=True, stop=True, tp=tp_(h))
                    for i in range(GS) for h in range(2)], mmk + [tkvk], [pk1])
                ACT(XVn[gi][:, 0:GS, :], ps1[:, 0:GS * 64].rearrange("p (i k) -> p i k", k=64), AF.Copy,
                    [pk1], [("XVn", gi)], scale=-1.0)
                ps2, pk2 = next_psc()
                MM([dict(out=ps2[hrow(h), i * C:(i + 1) * C], lhsT=TKV[hC(h), (g0 + i) * 4 + 3, :],
                         rhs=Ta[hC(h), g0 + i, 0:C], start=True, stop=True, tp=tp_(h))
                    for i in range(GS) for h in range(2)], [tkvk, tk_], [pk2])
                ACT(Ahat[db][:, g0:g0 + GS, 0:C], ps2[:, 0:GS * C].rearrange("p (i k) -> p i k", k=C), AF.Copy,
                    [pk2], [("Ahat", db, g0)], scale=-1.0)
                yield
                ps3, pk3 = next_psc()
                MM([dict(out=ps3[hC(h), i * 64:(i + 1) * 64], lhsT=Ta[hC(h), g0 + i, 0:C],
                         rhs=XVn[gi][hC(h), i, :], start=True, stop=True, tp=tp_(h))
                    for i in range(GS) for h in range(2)], [tk_, ("XVn", gi)], [pk3])
                COPY("dve", Vhat[db][:, g0:g0 + GS, :], ps3[:, 0:GS * 64].rearrange("p (i k) -> p i k", k=64),
                     [pk3], [("Vhat", db, g0)])
                yield

            def C_stage(fc):
                db = fc % 2
                a3 = fc % 3
                AR = ARs[a3]
                TKV, MMa = TKVall[db], MMall[db]
                for c in range(nch):
                    par = c % 2
                    g0 = (c // GS) * GS
                    cs = slice(c * C, (c + 1) * C)
                    BT, KT, VT = TKV[:, c * 4, :], TKV[:, c * 4 + 1, :], TKV[:, c * 4 + 2, :]
                    NB, MK = MMa[:, c, 0:2 * C], MMa[:, c, 2 * C:4 * C]
                    tkvk, mmk_ = ("TKV", db, g0), ("MM", db, c)
                    sbi = sbidx[seq][fc]
                    Sb_old, Sb_new = Sb[seq][sbi], Sb[seq][1 - sbi]
                    So_k, Sn_k = ("Sb", seq, sbi, fc), ("Sb", seq, 1 - sbi, fc)
                    sbidx[seq][fc] = 1 - sbi
                    UT = s_UT[par]
                    psU, pkU = next_psc()
                    MM([dict(out=psU[hC(h), 0:64], lhsT=Ahat[db][hrow(h), c, 0:C], rhs=Sb_old[hrow(h), fc, :],
                             start=True, stop=True, tp=tp_(h)) for h in range(2)],
                       [("Ahat", db, g0), So_k, ("Sb", seq, sbi)], [pkU])
                    TTOP("dve", UT[:, :], psU[:, 0:64], Vhat[db][:, c, :], ALU.add, [pkU, ("Vhat", db, g0)],
                         [("UT", par)])
                    yield
                    psS, pkS = next_psc()
                    MM([d for h in range(2) for d in (
                        dict(out=psS[hrow(h), 0:64], lhsT=BT[hC(h), :], rhs=UT[hC(h), :], start=True, stop=False,
                             tp=tp_(h)),
                        dict(out=psS[hrow(h), 0:64], lhsT=KT[hC(h), :], rhs=VT[hC(h), :], start=False, stop=True,
                             tp=tp_(h)))],
                       [tkvk, ("UT", par)], [pkS])
                    wc = wcb[a3][:, c:c + 1]
                    STT("dve", Sb_new[:, fc, :], Sf[seq][:, fc, :], wc, psS[:, 0:64], ALU.mult, ALU.add,
                        [pkS, ("Sf", seq, fc), ("Sf", seq), ("wcb", a3)], [Sn_k])
                    STT("dve", Sf[seq][:, fc, :], Sf[seq][:, fc, :], wc, psS[:, 0:64], ALU.mult, ALU.add,
                        [pkS, ("Sf", seq, fc), ("Sf", seq), ("wcb", a3)], [("Sf", seq, fc)])
                    psY, pkY = next_psc()
                    MM([d for h in range(2) for d in (
                        dict(out=psY[hrow(h), 0:C], lhsT=Sb_old[hrow(h), fc, :], rhs=AR[hrow(h), 1, cs], start=True, stop=False,
                             tp=tp_(h)),
                        dict(out=psY[hrow(h), 0:C], lhsT=UT[hC(h), :], rhs=NB[hC(h), C:2 * C], start=False, stop=False,
                             tp=tp_(h)),
                        dict(out=psY[hrow(h), 0:C], lhsT=VT[hC(h), :], rhs=MK[hC(h), C:2 * C], start=False, stop=True,
                             tp=tp_(h)))],
                       [("Rt", a3), So_k, ("Sb", seq, sbi), ("UT", par), mmk_, tkvk], [pkY])
                    COPY("act", t_ys[db][:, cs], psY[:, 0:C], [pkY], [("t_y", db, c)])
                    yield

            def G_stage(fc):
                db = fc % 2
                t_zs = t_zss[fc % 4]; t_bon = t_bons[fc % 4]
                zk = "t_zs%d" % (fc % 4)
                t_y = t_ys[db]
                ykeys = [("t_y", db, c) for c in range(nch)]
                gak = ["lmix", ("lmix", "c0")]
                ACT(g_b[:, sl], t_y[:, sl], AF.Square, ykeys, ["g_b"])
                psM, pkM = next_pm()
                MM([dict(out=psM[:, sl], lhsT=bmean, rhs=t_y[:, sl], start=True, stop=True)], ykeys + ["cf"], [pkM])
                ACT(g_a[:, sl], psM[:, sl], AF.Square, [pkM], gak)
                TTOP("dve", t_y[:, sl], t_y[:, sl], psM[:, sl], ALU.subtract, ykeys + [pkM], ykeys)
                yield
                psQ, pkQ = next_pm()
                MM([dict(out=psQ[:, sl], lhsT=bmean, rhs=g_b[:, sl], start=True, stop=True)], ["g_b", "cf"], [pkQ])
                TTOP("dve", g_a[:, sl], psQ[:, sl], g_a[:, sl], ALU.subtract, [pkQ] + gak, gak)
                ACT(g_a[:, sl], g_a[:, sl], AF.Ln, gak + ["epsb1"], gak, bias=GNEPS_AP[:, :])
                ACT(g_b[:, sl], g_a[:, sl], AF.Exp, gak + ["g_b"], ["g_b"], scale=-0.5)
                yield
                TTOP("pool", t_y[:, sl], t_y[:, sl], g_b[:, sl], ALU.mult, ykeys + ["g_b"], ykeys)
                STT("dve", t_y[:, sl], t_y[:, sl], vm(V_GNW + fc), t_bon[:, sl], ALU.mult, ALU.add,
                    ykeys + [("t_bon", fc % 4), "vecs"], ykeys)
                TTOP("pool", oa[:, fc, sl], t_y[:, sl], t_zs[:, sl], ALU.mult, ykeys + [zk], [("oa", fc)])
                yield

            def drain(*gens):
                for g_ in gens:
                    for _ in g_:
                        pass

            def chain_g(*gens):
                for g_ in gens:
                    for _ in g_:
                        yield

            def rr(gens):
                gens = list(gens)
                while gens:
                    for g_ in list(gens):
                        try:
                            next(g_)
                        except StopIteration:
                            gens.remove(g_)

            def par_g(*gens):
                gens = list(gens)
                while gens:
                    for g_ in list(gens):
                        try:
                            next(g_)
                            yield
                        except StopIteration:
                            gens.remove(g_)

            def I_stage(k):
                grp = list(range(0, nch, GS))
                return par_g(*[chain_g(I_pre(k, g0), I_inv(k, g0), I_post(k, g0)) for g0 in grp])

            def rrw(gws):
                gws = [[g_, w_, 0.0] for g_, w_ in gws]
                while gws:
                    for ent in list(gws):
                        ent[2] += ent[1]
                        while ent[2] >= 1.0:
                            ent[2] -= 1.0
                            try:
                                next(ent[0])
                            except StopIteration:
                                gws.remove(ent)
                                break

            for k in range(-1, 10):
                gl = []
                if 0 <= k - 1 <= 7:
                    gl.append((C_stage(k - 1), 1.0))
                if 0 <= k <= 7:
                    gl.append((I_stage(k), 1.65))
                if 0 <= k + 1 <= 7:
                    gl.append((P_stage(k + 1), 1.25))
                if 0 <= k - 2 <= 7:
                    gl.append((G_stage(k - 2), 0.2))
                rrw(gl)

            if seq == 1:
                DBG("oa", oa[:, :, sl], [("oa", f) for f in range(8)], BF16)
            ckpt("s2")
            for fc in range(8):
                wt, wk = next_w(G_B + fc)
                sl = slice(0, TTt)
                ps_zb, k_zb = proj_fm(wt, wk, 384)
                ACT(t_zb[:, sl], ps_zb[:, sl], AF.Silu, [k_zb], ["t_cs"])
                ps_gb, k_gb = proj_fm(wt, wk, 0)
                TTOP("dve", t_g1[:, sl], ps_gb[:, sl], t_zb[:, sl], ALU.mult, [k_gb, "t_cs"], ["t_winv"])
                ps_xb, k_xb = proj_fm(wt, wk, 256)
                COPY("act", t_xbs[:, sl], ps_xb[:, sl], [k_xb], ["t_sg"])
                COPY("pool", ubuf[:, 0:2], convc[seq][:, fc, :], [("convc", seq), ("convc", seq, fc)], ["t_kap"])
                ps_gc, k_gc = proj_fm(wt, wk, 128)
                TTOP("dve", ubuf[:, 2:TTt + 2], ps_gc[:, sl], t_xbs[:, sl], ALU.mult, [k_gc, "t_sg"], ["t_kap"])
                COPY("pool", convc[seq][:, fc, :], ubuf[:, TTt:TTt + 2], ["t_kap", "t_kap"], [("convc", seq, fc)])
                ACT(t_c1[:, sl], ubuf[:, 0:TTt], AF.Copy, ["t_kap", "t_kap", "vecs"], ["t_a"], scale=vm(V_CONV + fc))
                STT("dve", t_c1[:, sl], ubuf[:, 1:TTt + 1], vm(V_CONV + 8 + fc), t_c1[:, sl], ALU.mult, ALU.add,
                    ["t_kap", "t_kap", "t_a", "vecs"], ["t_a"])
                STT("dve", t_c1[:, sl], ubuf[:, 2:TTt + 2], vm(V_CONV + 16 + fc), t_c1[:, sl], ALU.mult, ALU.add,
                    ["t_kap", "t_a", "vecs"], ["t_a"])
                TTOP("pool", ob[:, fc, sl], t_g1[:, sl], t_c1[:, sl], ALU.mult, ["t_winv", "t_a"], [("ob", fc)])

            if seq == 1:
                DBG("ob", ob[:, :, sl], [("ob", f) for f in range(8)], BF16)
            ckpt("s3")
            oa_keys = [("oa", f) for f in range(8)]
            ob_keys = [("ob", f) for f in range(8)]
            sl = slice(0, TTt)
            for g in range(2):
                woa, woak = next_w(G_WOA + g)
                wga, wgak = next_w(G_GA + g)
                for i in range(4):
                    oc = 4 * g + i
                    ps_o, k_o = proj_fm(woa, woak, i * 128, rhsT=oa, rkeys=oa_keys)
                    ps_g, k_g = proj_fm(wga, wgak, i * 128)
                    ACT(t_sga[:, sl], ps_g[:, sl], AF.Sigmoid, [k_g], ["t_wx"])
                    TTOP("dve", m1[:, oc, sl], ps_o[:, sl], t_sga[:, sl], ALU.mult, [k_o, "t_wx"], [("m1", oc)] + ALIAS_K)
            for g in range(2):
                wob, wobk = next_w(G_WOB + g)
                wgb, wgbk = next_w(G_GB + g)
                for i in range(4):
                    oc = 4 * g + i
                    ps_o, k_o = proj_fm(wob, wobk, i * 128, rhsT=ob, rkeys=ob_keys)
                    ps_g, k_g = proj_fm(wgb, wgbk, i * 128)
                    ACT(t_sga[:, sl], ps_g[:, sl], AF.Sigmoid, [k_g], ["t_wx"])
                    TTOP("dve", t_m2b[:, sl], ps_o[:, sl], t_sga[:, sl], ALU.mult, [k_o, "t_wx"], ["t_kt"])
                    TTOP("pool", oa[:, oc, sl], t_m2b[:, sl], m1[:, oc, sl], ALU.add, ["t_kt", ("m1", oc)] + ALIAS_K,
                         [("oa", oc)])
            m_keys = oa_keys

            if seq == 1:
                DBG("m", oa[:, :, sl], [("oa", f) for f in range(8)], BF16)
            ckpt("s4")
        if "tail" in phases:
            for g in range(2):
                wo, wok = next_w(G_WOUT + g)
                for tb in range(nblk):
                    ps, pk = next_pm()
                    MM([dict(out=ps[0:bt, :], lhsT=oa[:, kc, tb * bt:(tb + 1) * bt], rhs=wo[:, kc, :],
                             start=(kc == 0), stop=(kc == 7)) for kc in range(8)], [wok] + m_keys, [pk])
                    dst = xtb[0:bt, tb, g * 512:(g + 1) * 512]
                    TTOP("dve", dst, ps[0:bt, :], dst, ALU.add, [pk, xk], [("xv", tb, g)])
                    yield
            ckpt("s5")
            yield from par_t(*[rms_to_T(xtb[0:bt, tb, :], V_GPLE, ob, "ob", tb, [xk, ("xv", tb, 0), ("xv", tb, 1)],
                                        coarse=True, si=tb) for tb in range(nblk)])
            h2T_keys = [("ob", kc) for kc in range(8)]
            for g in range(2):
                wg_, wgk_ = next_w(G_WPG + g)
                wp_, wpk_ = next_w(G_WPLE + g)
                for tb in range(nblk):
                    psg, pkg = next_pm()
                    MM([dict(out=psg[0:bt, :], lhsT=ob[:, kc, tb * bt:(tb + 1) * bt], rhs=wg_[:, kc, :],
                             start=(kc == 0), stop=(kc == 7)) for kc in range(8)], [wgk_] + h2T_keys, [pkg])
                    psp, pkp = next_pm()
                    MM([dict(out=psp[0:bt, :], lhsT=pT[:, kc, tb * bt:(tb + 1) * bt], rhs=wp_[:, kc, :],
                             start=(kc == 0), stop=(kc == 1)) for kc in range(2)], [wpk_] + pT_keys, [pkp])
                    tf_, tfk_ = ((t_t1, "t_t1"), (t_kt, "t_kt"))[tb % 2]
                    ACT(tf_[0:bt, :], psg[0:bt, :], AF.Sigmoid, [pkg], [tfk_])
                    TTOP("dve", tf_[0:bt, :], tf_[0:bt, :], psp[0:bt, :], ALU.mult, [tfk_, pkp], [tfk_])
                    dst = xtb[0:bt, tb, g * 512:(g + 1) * 512]
                    TTOP("pool", dst, dst, tf_[0:bt, :], ALU.add, [tfk_, xk, ("xv", tb, g)],
                         [("xv", tb, g)])
                    yield
            ckpt("s6")
            def fin_tb(tb):
                src = xtb[0:bt, tb, :]
                rd = [xk, ("xv", tb, 0), ("xv", tb, 1)]
                yield from norm_stats(src, rd, tb)
                c0_ = NSETS[tb][2]
                STT("dve", src, src, stat[0:bt, c0_ + 2:c0_ + 3], gfin[0:bt, :], ALU.mult, ALU.mult,
                    rd + ["rstd%d" % tb, "gfin"], [("xv", tb, 0), ("xv", tb, 1)])
                yield
                DMA("act", y_d[tok0 + tb * bt:tok0 + (tb + 1) * bt, :], src, [("xv", tb, 0), ("xv", tb, 1)],
                    [("y_out", seq, tix, tb), ("xt_store", tix % 2, tb)], "yo%d_%d" % (tix % 2, tb))
                yield

            yield from par_t(*[fin_tb(tb) for tb in range(nblk)])
            if last:
                fk = [("Sf", seq, f) for f in range(8)]
                DMA("act", o_wkv[seq][:], Sf[seq][:], fk + [("Sf", seq)], [("o_wkv", seq)], "so")
                DMA("act", o_sh[seq][:], shout[seq][:], [("shout", seq, i) for i in range(NSH)],
                    [("o_sh", seq)], "so")
                DMA("act", o_cv[seq][:], convc[seq][:], [("convc", seq, f) for f in range(8)] + [("convc", seq)],
                    [("o_cv", seq)], "so")

    sbidx = [[0] * 8, [0] * 8]

    def xk_dummy(tix):
        return ("xt_store", tix % 2)

    def rr_top(gens):
        gens = list(gens)
        while gens:
            for g_ in list(gens):
                try:
                    next(g_)
                except StopIteration:
                    gens.remove(g_)

    spos = SAMPLE_POS_DEFAULT
    spos = max(0, min(spos, n_prompt_tiles))
    tiles = [("p", t) for t in range(n_prompt_tiles)]
    tiles.insert(spos, ("s", 0))

    def mk(pos, phases):
        kind, t = tiles[pos]
        if kind == "p":
            return run_tile(0, pos, xP, pP, yP, t * TT_P, TT_P, 64, t == n_prompt_tiles - 1, phases)
        return run_tile(1, pos, xS, pS, yS, 0, TS, 16, True, phases)

    rr_top([mk(0, ("head",))])
    for pos in range(len(tiles)):
        rr_top([mk(pos, ("mid",))])
        gl = [mk(pos, ("tail",))]
        if pos + 1 < len(tiles):
            gl.append(mk(pos + 1, ("head",)))
        rr_top(gl)

    P.schedule()
    sems = {}
    for k in P.sem_keys():
        sems[k] = es.enter_context(nc.semaphore(k.replace(":", "_")))
    block = es.enter_context(nc.Block())

    def replay(e, eng, final=False):
        for waits, fn, inc in P.q[eng]:
            for sk, v in waits:
                e.wait_ge(sems[sk], v)
            ins = fn(e)
            ins.then_inc(sems[inc[0]], inc[1])
        if final:
            for sk, v in P.dcnt.items():
                e.wait_ge(sems[sk], v)

    @block.tensor
    def _(e):
        replay(e, "pe")

    @block.scalar
    def _(e):
        replay(e, "act", final=True)

    @block.vector
    def _(e):
        replay(e, "dve")

    @block.gpsimd
    def _(e):
        replay(e, "pool")

    @block.sync
    def _(e):
        replay(e, "sp")

    es.close()
    return nc


_CACHE = {}


def _consts():
    cbf = np.zeros((128, 320), np.float32)
    cbf[:, 0:128] = np.eye(128)
    for h in range(2):
        cbf[64 * h:64 * h + 64, 128:192] = np.eye(64)
        cbf[64 * h:64 * h + 64, 192 + 64 * h:192 + 64 * h + 64] = 1.0
    W = 128 + 64 + 5 * 128 + 5 * 32 + TT_P + TS
    cf = np.zeros((128, W), np.float32)
    cf[:, 0:128] = cbf[:, 192:320] / 64.0
    cf[:, 128:192] = cbf[:, 128:192]
    o = 192
    for C, base, w in ((64, o, 128), (16, o + 640, 32)):
        s = (np.arange(128) % 64)[:, None]
        t = np.arange(C)[None, :]
        msu = (s < t).astype(np.float32)
        miu = (s <= t).astype(np.float32)
        msl = (t < s).astype(np.float32)
        cf[:, base:base + w] = np.concatenate([-msu, miu], 1)
        cf[:, base + w:base + 2 * w] = np.concatenate([msu, miu], 1)
        cf[:, base + 2 * w:base + 2 * w + C] = -msl
    o2 = o + 640 + 160
    seg = np.ones(TT_P, np.float32); seg[::64] = 0.0
    cf[:, o2:o2 + TT_P] = seg[None, :]
    seg2 = np.ones(TS, np.float32); seg2[0] = 0.0
    cf[:, o2 + TT_P:o2 + TT_P + TS] = seg2[None, :]
    return cbf.astype(ml_dtypes.bfloat16), cf


def _fm(v):
    v = np.asarray(v, np.float32).reshape(-1)
    return np.ascontiguousarray(v.reshape(-1, 128).T)


def kernel(x_prompt, x_sample, state_wkv, state_shift, state_conv, p_prompt, p_sample,
           g_norm, w_in, mu_shift, w_decay0, w_decay2, w_iclr0, w_iclr2, k_removal,
           k_replace, r_bonus, gn_w, gn_b, conv_w, w_o_a, w_o_b, w_out, g_ple,
           w_ple_gate, w_ple, g_final, _n_prompt_tiles=TP // TT_P, _stop=None, _debug=False):
    f = lambda a: np.ascontiguousarray(np.asarray(a, np.float32))
    nt = _n_prompt_tiles
    if (nt, _stop) not in _CACHE:
        _CACHE[(nt, _stop)] = build_program(nt, _stop, debug=_debug)
    nc = _CACHE[(nt, _stop)]
    cbf, cf = _consts()
    vecs = np.zeros((128, NV), np.float32)
    vecs[:, V_GNORM:V_GNORM + 8] = _fm(g_norm[0])
    vecs[:, V_MU:V_MU + NSH] = _fm(mu_shift[0])
    vecs[:, V_WD0:V_WD0 + 8] = _fm(w_decay0[0])
    vecs[:, V_WI0:V_WI0 + 8] = _fm(w_iclr0[0])
    vecs[:, V_KREM:V_KREM + 8] = _fm(k_removal[0])
    vecs[:, V_KREP:V_KREP + 8] = _fm(k_replace[0])
    vecs[:, V_GNW:V_GNW + 8] = _fm(gn_w[0])
    vecs[:, V_GNB:V_GNB + 8] = _fm(gn_b[0])
    for j in range(3):
        vecs[:, V_CONV + 8 * j:V_CONV + 8 * j + 8] = _fm(conv_w[0, j])
    vecs[:, V_GPLE:V_GPLE + 8] = _fm(g_ple[0])
    vecs[:, V_RBON:V_RBON + 8] = _fm(r_bonus[0])
    gfin_bc = np.ascontiguousarray(np.broadcast_to(f(g_final)[None, :], (128, D)))
    w_lora = np.ascontiguousarray(np.concatenate([f(w_decay2[0]), f(w_iclr2[0])], 0))
    shared = {
        "w_in": f(w_in[0]), "w_o_a": f(w_o_a[0]), "w_o_b": f(w_o_b[0]), "w_out": f(w_out[0]),
        "w_ple_gate": f(w_ple_gate[0]), "w_ple": f(w_ple[0]), "w_lora": w_lora,
        "vecs": vecs, "gfin_bc": gfin_bc, "cbf": cbf, "cf32": cf,
    }
    in_maps = []
    for c in range(8):
        wkv = f(state_wkv[0, c])
        wkvT = wkv.reshape(8, 2, 64, 64).transpose(1, 3, 0, 2).reshape(128, 8, 64)
        sh = _fm(state_shift[0, c])
        cv = f(state_conv[0, c])
        cvf = np.ascontiguousarray(cv.reshape(2, 8, 128).transpose(2, 1, 0))
        m = dict(shared)
        m.update({
            "x_p": f(x_prompt[c]), "x_s": f(x_sample[c]), "p_p": f(p_prompt[0, c]), "p_s": f(p_sample[0, c]),
            "wkv0T": np.ascontiguousarray(wkvT), "shift0": sh, "conv0": cvf,
        })
        in_maps.append(m)
    res = run_bass_kernel_spmd(nc, in_maps, core_ids=list(range(8)))
    R = res.results
    if _debug:
        global DBG_OUT
        DBG_OUT = [{n: np.asarray(R[c][n]) for n in DBG_LIST} for c in range(8)]

    def un_wkv(a):
        return np.ascontiguousarray(a.reshape(2, 64, 8, 64).transpose(2, 0, 3, 1).reshape(16, 64, 64))

    def un_fm(a):
        return np.ascontiguousarray(a.T.reshape(-1))

    def un_cv(a):
        return np.ascontiguousarray(a.transpose(2, 1, 0).reshape(2, 1024))

    y_p = np.stack([R[c]["y_p"] for c in range(8)]).astype(np.float32)
    y_s = np.stack([R[c]["y_s"] for c in range(8)]).astype(np.float32)
    wkv_p = np.stack([un_wkv(R[c]["wkv_p"]) for c in range(8)])[None].astype(np.float32)
    sh_p = np.stack([un_fm(R[c]["shift_p"]) for c in range(8)])[None].astype(np.float32)
    cv_p = np.stack([un_cv(R[c]["conv_p"]) for c in range(8)])[None].astype(np.float32)
    wkv_s = np.stack([un_wkv(R[c]["wkv_s"]) for c in range(8)])[None].astype(np.float32)
    sh_s = np.stack([un_fm(R[c]["shift_s"]) for c in range(8)])[None].astype(np.float32)
    cv_s = np.stack([un_cv(R[c]["conv_s"]) for c in range(8)])[None].astype(np.float32)
    return (y_p, y_s, wkv_p, sh_p, cv_p, wkv_s, sh_s, cv_s)
```

```python
import contextlib
import numpy as np
import ml_dtypes
import concourse.bass as bass
import concourse.mybir as mybir
from concourse.bass_utils import run_bass_kernel_spmd

F32 = mybir.dt.float32
BF16 = mybir.dt.bfloat16
AF = mybir.ActivationFunctionType
ALU = mybir.AluOpType
AX = mybir.AxisListType

D = 1024
TP = 4096
TS = 16
NH = 16
HD = 64
NFC = 8
PLE = 256
EPS = 1e-6
GN_EPS = 64e-5
DS = float(np.exp(-0.5))
SHIFT_COLS = 4224
NSH = 33
IN_COLS = 10368
TT_P = 512
NSLOT = 3
LOOKAHEAD = 1
SAMPLE_POS_DEFAULT = 8

OFF_R, OFF_K, OFF_V, OFF_LR, OFF_ZA = 0, 1024, 2048, 3072, 3200
OFF_GB, OFF_GC, OFF_XB, OFF_ZB = 4224, 5248, 6272, 7296
OFF_GA, OFF_GBm = 8320, 9344

G_LORA = 0
G_A = 1
G_B = 9
G_GA = 17
G_GB = 19
G_WOA = 21
G_WOB = 23
G_WOUT = 25
G_WPG = 27
G_WPLE = 29
NGRP = 31

V_GNORM = 0
V_MU = 8
V_WD0 = 41
V_WI0 = 49
V_KREM = 57
V_KREP = 65
V_GNW = 73
V_GNB = 81
V_CONV = 89
V_GPLE = 113
V_RBON = 121
NV = 129


class Prog:
    ENGS = ("pe", "act", "dve", "pool", "sp")
    LAT = 0.35

    def __init__(self):
        self.ops = []
        self.lastw = {}
        self.readers = {}
        self.fences = []
        self.last_tok = None
        self.disabled = False

    def op(self, eng, fn, reads=(), writes=(), dma=None, dur=None, xfer=0.0):
        if self.disabled:
            return
        bank_r = [k for k in reads if isinstance(k, tuple) and k[0] == "BANK"]
        if bank_r:
            reads = [k for k in reads if not (isinstance(k, tuple) and k[0] == "BANK")]
            writes = list(writes) + bank_r
        idx = len(self.ops)
        deps = {}
        for f in self.fences:
            deps[f] = True
        for k in reads:
            t = self.lastw.get(k)
            if t is not None:
                deps[t] = True
        for k in writes:
            t = self.lastw.get(k)
            if t is not None:
                deps[t] = True
            for t in self.readers.get(k, ()):
                if t not in deps:
                    deps[t] = False
        self.ops.append(dict(eng=eng, fn=fn, deps=deps, dma=dma, dur=(0.3 if dur is None else dur), xfer=xfer,
                             lab=(str(writes[0]) if len(writes) else "-")))
        self.last_tok = idx
        for k in writes:
            self.lastw[k] = idx
            self.readers[k] = []
        for k in reads:
            self.readers.setdefault(k, []).append(idx)

    def schedule(self):
        import heapq
        ops = self.ops
        n = len(ops)
        succ = [[] for _ in range(n)]
        indeg = [0] * n
        for i, o in enumerate(ops):
            indeg[i] = len(o["deps"])
            for d in o["deps"]:
                succ[d].append(i)
        prio = [0.0] * n
        for i in range(n - 1, -1, -1):
            m = 0.0
            for sidx in succ[i]:
                if prio[sidx] > m:
                    m = prio[sidx]
            prio[i] = ops[i]["dur"] + ops[i]["xfer"] + m
        finish = [0.0] * n
        dready = [0.0] * n
        pend = {e: [] for e in self.ENGS}
        avail = {e: [] for e in self.ENGS}
        efree = {e: 0.0 for e in self.ENGS}
        order = {e: [] for e in self.ENGS}
        for i in range(n):
            if indeg[i] == 0:
                heapq.heappush(pend[ops[i]["eng"]], (0.0, i))
        done = 0
        while done < n:
            best = None
            for e in self.ENGS:
                pe_, av = pend[e], avail[e]
                while pe_ and pe_[0][0] <= efree[e]:
                    _, i = heapq.heappop(pe_)
                    heapq.heappush(av, (-prio[i], i))
                if av:
                    cand = (efree[e], 0, e)
                elif pe_:
                    cand = (pe_[0][0], 1, e)
                else:
                    continue
                if best is None or cand < best:
                    best = cand
            t, kind, e = best
            if kind == 0:
                _, i = heapq.heappop(avail[e])
            else:
                _, i = heapq.heappop(pend[e])
            start = max(efree[e], dready[i])
            o = ops[i]
            o["start"] = start
            o["why"] = ("eng" if efree[e] >= dready[i] else "dep")
            efree[e] = start + o["dur"]
            finish[i] = start + o["dur"] + o["xfer"]
            order[e].append(i)
            done += 1
            for sidx in succ[i]:
                same = ops[sidx]["eng"] == e and o["dma"] is None
                r = finish[i] + (0.0 if same else self.LAT)
                if r > dready[sidx]:
                    dready[sidx] = r
                indeg[sidx] -= 1
                if indeg[sidx] == 0:
                    heapq.heappush(pend[ops[sidx]["eng"]], (dready[sidx], sidx))
        self.order = order
        self.makespan = max(finish) if n else 0.0
        self.tok = [None] * n
        self.dcnt = {}
        for e in self.ENGS:
            c = 0
            for i in order[e]:
                o = ops[i]
                if o["dma"] is None:
                    c += 1
                    self.tok[i] = ("e:" + e, c)
                else:
                    sk = "dma:" + o["dma"]
                    self.dcnt[sk] = self.dcnt.get(sk, 0) + 16
                    self.tok[i] = (sk, self.dcnt[sk])
        self.q = {e: [] for e in self.ENGS}
        for e in self.ENGS:
            seen = {}
            for i in order[e]:
                o = ops[i]
                waits = {}
                for d, need in o["deps"].items():
                    od = ops[d]
                    if od["eng"] == e and od["dma"] is None and not need and o["dma"] is None:
                        continue
                    sk, val = self.tok[d]
                    if seen.get(sk, 0) >= val:
                        continue
                    if waits.get(sk, 0) < val:
                        waits[sk] = val
                for sk, v in waits.items():
                    seen[sk] = v
                inc = (self.tok[i][0], 16 if o["dma"] is not None else 1)
                self.q[e].append((list(waits.items()), o["fn"], inc))

    def sem_keys(self):
        ks = ["e:" + e for e in self.ENGS]
        ks += sorted(self.dcnt.keys())
        return ks


class _Stop(Exception):
    pass


DBG_LIST = []


def build_program(n_prompt_tiles=TP // TT_P, stop=None, debug=False):
    del DBG_LIST[:]
    def ckpt(name):
        if stop is not None and name == stop:
            P.disabled = True
    nc = bass.Bass("TRN2", target_bir_lowering=False)
    P = Prog()
    es = contextlib.ExitStack()

    def dram(name, shape, dt, kind):
        return nc.dram_tensor(name, list(shape), dt, kind=kind).ap()

    xP = dram("x_p", [TP, D], F32, "ExternalInput")
    xS = dram("x_s", [TS, D], F32, "ExternalInput")
    pP = dram("p_p", [TP, PLE], F32, "ExternalInput")
    pS = dram("p_s", [TS, PLE], F32, "ExternalInput")
    wkv0 = dram("wkv0T", [128, NFC, HD], F32, "ExternalInput")
    sh0 = dram("shift0", [128, NSH], F32, "ExternalInput")
    cv0 = dram("conv0", [128, NFC, 2], F32, "ExternalInput")
    w_in = dram("w_in", [D, IN_COLS], F32, "ExternalInput")
    w_oa = dram("w_o_a", [D, D], F32, "ExternalInput")
    w_ob = dram("w_o_b", [D, D], F32, "ExternalInput")
    w_out = dram("w_out", [D, D], F32, "ExternalInput")
    w_pg = dram("w_ple_gate", [D, D], F32, "ExternalInput")
    w_ple = dram("w_ple", [PLE, D], F32, "ExternalInput")
    w_lora = dram("w_lora", [128, D], F32, "ExternalInput")
    vecs_d = dram("vecs", [128, NV], F32, "ExternalInput")
    gfin_d = dram("gfin_bc", [128, D], F32, "ExternalInput")
    cbf_d = dram("cbf", [128, 320], BF16, "ExternalInput")
    cf_d = dram("cf32", [128, 128 + 64 + 5 * 128 + 5 * 32 + TT_P + TS], F32, "ExternalInput")

    yP = dram("y_p", [TP, D], F32, "ExternalOutput")
    yS = dram("y_s", [TS, D], F32, "ExternalOutput")
    o_wkv = [dram("wkv_p", [128, NFC, HD], F32, "ExternalOutput"),
             dram("wkv_s", [128, NFC, HD], F32, "ExternalOutput")]
    o_sh = [dram("shift_p", [128, NSH], F32, "ExternalOutput"),
            dram("shift_s", [128, NSH], F32, "ExternalOutput")]
    o_cv = [dram("conv_p", [128, NFC, 2], F32, "ExternalOutput"),
            dram("conv_s", [128, NFC, 2], F32, "ExternalOutput")]
    wsc = dram("wsc", [NGRP, 128, 8, 512], BF16, "Internal")

    def sb(name, shape, dt=F32):
        return es.enter_context(nc.sbuf_tensor("sb_" + name, list(shape), dt))

    def psum(name, shape, dt=F32):
        return es.enter_context(nc.psum_tensor("ps_" + name, list(shape), dt))

    TT = TT_P
    vecs = sb("vecs", [128, NV])
    omm = sb("omm", [128, NSH])
    omkr = sb("omkr", [128, 8])
    gfin = sb("gfin", [128, D])
    cbf = sb("cbf", [128, 320], BF16)
    cf = sb("cf", [128, 128 + 64 + 5 * 128 + 5 * 32 + TT_P + TS])
    wlora = sb("wlora", [128, D], BF16)
    RB = sb("RB", [128, 8, 128], BF16)
    ident = cbf[:, 0:128]
    sident = cbf[:, 128:192]
    bones = cbf[:, 192:320]
    bmean = cf[:, 0:128]
    sidf = cf[:, 128:192]
    o = 192
    maskcat = {64: cf[:, o:o + 320], 16: cf[:, o + 640:o + 720]}
    maskA = {64: cf[:, o:o + 128], 16: cf[:, o + 640:o + 672]}
    maskB = {64: cf[:, o + 128:o + 256], 16: cf[:, o + 672:o + 704]}
    maskL = {64: cf[:, o + 256:o + 320], 16: cf[:, o + 704:o + 720]}
    o2 = o + 640 + 160
    segm = {TT_P: cf[:, o2:o2 + TT_P], TS: cf[:, o2 + TT_P:o2 + TT_P + TS]}

    xt = [sb("xt0", [128, TT // 128, D]), sb("xt1", [128, TT // 128, D])]
    xnb = sb("xnb", [128, D], BF16)
    sq_junk = xnb
    sh0b = sb("sh0b", [128, NSH])
    stat = sb("stat", [128, 32])
    hT = sb("hT", [128, 8, TT], BF16)
    pT = sb("pT", [128, 2, TT], BF16)
    wbuf = [sb("wbuf%d" % i, [128, 8, 512], BF16) for i in range(NSLOT)]
    oa = sb("oa", [128, 8, TT], BF16)
    ob = sb("ob", [128, 8, TT], BF16)
    lrb = sb("lrb", [128, TT], BF16)
    lmix = sb("lmix", [128, TT])
    Sf = [sb("Sf%d" % i, [128, NFC, HD]) for i in range(2)]
    Sb = [[sb("Sb%d_%d" % (i, j), [128, NFC, HD], BF16) for j in range(2)] for i in range(2)]
    carry = [sb("carry%d" % i, [128, NSH]) for i in range(2)]
    convc = [sb("convc%d" % i, [128, NFC, 2]) for i in range(2)]
    shout = [sb("shout%d" % i, [128, NSH]) for i in range(2)]
    Bm = [sb("Bm%d" % i, [128, TT]) for i in range(2)]
    t_r = sb("t_r", [128, TT]); t_k = sb("t_k", [128, TT]); t_v = sb("t_v", [128, TT])
    t_zss = [sb("t_zs%d" % i, [128, TT], BF16) for i in range(4)]
    t_zraw = None
    t_sg = sb("t_sg", [128, TT]); t_a = sb("t_a", [128, TT]); t_cs = sb("t_cs", [128, TT])
    t_wi = sb("t_wi", [128, TT]); t_winv = sb("t_winv", [128, TT]); t_wx = sb("t_wx", [128, TT])
    t_kap = sb("t_kap", [128, TT + 2]); t_ksq = sb("t_ksq", [128, TT], BF16)
    t_nrm = sb("t_nrm", [128, TT]); t_kh = sb("t_kh", [128, TT]); t_beta = sb("t_beta", [128, TT])
    t_t1 = sb("t_t1", [128, TT]); t_kt = sb("t_kt", [128, TT])
    ARs = [sb("AR%d" % i, [128, 2, TT], BF16) for i in range(3)]
    Bts = [sb("Bt%d" % i, [128, TT], BF16) for i in range(2)]
    Kts = [sb("Kt%d" % i, [128, TT], BF16) for i in range(2)]
    Vb1 = sb("Vb", [128, TT], BF16)
    Vbs = [Vb1, Vb1]
    t_rk = t_ksq
    t_bons = [sb("t_bon%d" % i, [128, TT], BF16) for i in range(4)]
    t_ys = [sb("t_y%d" % i, [128, TT]) for i in range(2)]
    g_a = lmix
    g_b = sb("g_b", [128, TT])
    pb = g_b[:, :].bitcast(BF16).rearrange("p (b d) -> p b d", d=PLE)
    ubuf = t_kap
    t_xbs, t_c1, t_zb, t_g1, t_sga, t_m2b, t_fin = t_sg, t_a, t_cs, t_winv, t_wx, t_kt, t_t1
    NCH = TT // 64
    scanbig = sb("scanbig", [128, 4608], BF16)
    TKVall = [scanbig[:, 2560:4608].rearrange("p (j k) -> p j k", k=64),
              sb("TKVall1", [128, NCH * 4, 64], BF16)]
    MMall = [scanbig[:, 0:2560].rearrange("p (c k) -> p c k", k=320),
             sb("MMall1", [128, NCH, 320], BF16)]
    m1 = scanbig[:, 0:4096].rearrange("p (c k) -> p c k", k=TT)
    ALIAS_K = [("MM", 0, c) for c in range(NCH)] + [("TKV", 0, 0), ("TKV", 0, 4)]
    Tall0 = sb("Tall0", [128, NCH, 64], BF16)
    Tall = [Tall0, Tall0]
    Ahat = [sb("Ahat%d" % i, [128, NCH, 64], BF16) for i in range(2)]
    Vhat = [sb("Vhat%d" % i, [128, NCH, 64], BF16) for i in range(2)]
    XVn = [sb("XVn%d" % i, [128, 4, 64], BF16) for i in range(2)]
    wcb = [sb("wcb%d" % i, [128, NCH]) for i in range(3)]
    PPg = [[sb("PPg%d_%d" % (g, i), [128, 4, 128], BF16) for i in range(2)] for g in range(2)]
    Tg = [[sb("Tg%d_%d" % (g, i), [128, 4, 64], BF16) for i in range(2)] for g in range(2)]
    Bt2s = [sb("Bt2_%d" % i, [128, TT], BF16) for i in range(2)]
    Kt2s = [sb("Kt2_%d" % i, [128, TT], BF16) for i in range(2)]
    s_UT = [sb("s_UT%d" % i, [128, 64], BF16) for i in range(2)]

    pm = [psum("pm%d" % i, [128, 512]) for i in range(3)]
    psc = [psum("psc%d" % i, [128, 512]) for i in range(4)]
    ptb = psum("ptb", [128, 1024], BF16)
    PTB_K = ("BANK", 7)

    def _n(ap):
        try:
            return int(ap.free_size())
        except Exception:
            return 256

    def _dur(eng, n):
        if eng == "act":
            return 1.25 * (0.22 + n / 1350.0)
        if eng == "dve":
            return 1.12 * (0.07 + n / 850.0)
        if eng == "pool":
            return 1.15 * (0.10 + n / 440.0)
        return 0.3

    def ACT(out, in_, func, reads, writes, bias=None, scale=None, accum=None):
        kw = {}
        if bias is not None: kw["bias"] = bias
        if scale is not None: kw["scale"] = scale
        if accum is not None: kw["accum_out"] = accum
        P.op("act", lambda e: e.activation(out=out, in_=in_, func=func, **kw), reads, writes, dur=_dur("act", _n(out)))

    def TTOP(eng, out, in0, in1, op, reads, writes):
        P.op(eng, lambda e: e.tensor_tensor(out, in0, in1, op), reads, writes, dur=_dur(eng, _n(out)))

    def TS_(eng, out, in0, s1, s2, op0, op1, reads, writes):
        if s2 is None:
            P.op(eng, lambda e: e.tensor_scalar(out, in0, s1, None, op0), reads, writes, dur=_dur(eng, _n(out)))
        else:
            P.op(eng, lambda e: e.tensor_scalar(out, in0, s1, s2, op0, op1), reads, writes, dur=_dur(eng, _n(out)))

    def STT(eng, out, in0, scalar, in1, op0, op1, reads, writes):
        P.op(eng, lambda e: e.scalar_tensor_tensor(out, in0, scalar, in1, op0, op1), reads, writes,
             dur=_dur(eng, _n(out)))

    def COPY(eng, out, in_, reads, writes):
        if eng == "act":
            P.op("act", lambda e: e.copy(out, in_), reads, writes, dur=_dur("act", _n(out)))
        else:
            P.op(eng, lambda e: e.tensor_copy(out, in_), reads, writes, dur=_dur(eng, _n(out)))

    def DMA(eng, out, in_, reads, writes, key):
        try:
            nb = float(out.nbytes())
        except Exception:
            nb = 65536.0
        P.op(eng, lambda e: e.dma_start(out=out, in_=in_), reads, writes, dma=key,
             dur=(1.0 if eng == "pool" else 0.08), xfer=2.0 + nb / 1.6e5)

    def MM(mms, reads, writes):
        def fn(e):
            ins = None
            for m in mms:
                kw = {}
                if m.get("tp") is not None: kw["tile_position"] = m["tp"]
                ins = e.matmul(m["out"], lhsT=m["lhsT"], rhs=m["rhs"], start=m["start"], stop=m["stop"], **kw)
            return ins
        d = 0.03
        for m in mms:
            d += max(0.032, _n(m["rhs"]) / 1800.0)
        P.op("pe", fn, reads, writes, dur=d)

    def TR(trs, reads, writes):
        def fn(e):
            ins = None
            for t in trs:
                kw = {}
                if t.get("tp") is not None: kw["tile_position"] = t["tp"]
                ins = e.transpose(t["out"], t["in_"], t["ident"], **kw)
            return ins
        P.op("pe", fn, reads, writes, dur=0.15 + 0.06 * len(trs))

    rot = {"pm": 0, "psc": 0}

    allb = pm + psc

    def next_pm():
        i = rot["pm"]; rot["pm"] = (i + 1) % 7
        return allb[i], ("BANK", i)

    def next_psc():
        return next_pm()

    def DBG(name, ap, keys, dt=F32):
        if not debug or getattr(P, "disabled", False):
            return
        shp = list(ap.shape)
        d = nc.dram_tensor("dbg_" + name, shp, dt, kind="ExternalOutput").ap()
        DBG_LIST.append("dbg_" + name)
        DMA("sp", d, ap, keys, [("dbg", name)], "dbg")

    epsb = sb("epsb", [128, 2])
    P.op("pool", lambda e: e.memset(epsb[:, 0:1], EPS), [], ["epsb0"])
    P.op("pool", lambda e: e.memset(epsb[:, 1:2], GN_EPS), [], ["epsb1"])
    EPS_AP = epsb[:, 0:1]
    GNEPS_AP = epsb[:, 1:2]
    DMA("sp", vecs[:], vecs_d[:], [], ["vecs"], "c0a")
    DMA("sp", gfin[:], gfin_d[:], [], ["gfin"], "c0b")
    DMA("sp", cbf[:], cbf_d[:], [], ["cbf"], "c0c")
    DMA("sp", cf[:], cf_d[:], [], ["cf"], "c0d")
    DMA("pool", wlora[:], w_lora[:], [], ["wlora"], "c1")
    TS_("dve", omm[:], vecs[:, V_MU:V_MU + NSH], -1.0, 1.0, ALU.mult, ALU.add, ["vecs"], ["omm"])
    TS_("dve", omkr[:], vecs[:, V_KREP:V_KREP + 8], -1.0, 1.0, ALU.mult, ALU.add, ["vecs"], ["omkr"])
    for fc in range(8):
        TS_("dve", RB[:, fc, :], bones, vecs[:, V_RBON + fc:V_RBON + fc + 1], None, ALU.mult, None,
            ["vecs", "cbf"], [("RB", fc)])
    P.op("pool", lambda e: e.memset(Sf[0][:], 0.0), [], [("Sf", 0)])
    P.op("pool", lambda e: e.memset(Sb[0][0][:], 0.0), [], [("Sb", 0, 0)])
    P.op("pool", lambda e: e.memset(carry[0][:], 0.0), [], [("carry", 0)])
    P.op("pool", lambda e: e.memset(convc[0][:], 0.0), [], [("convc", 0)])
    DMA("sp", Sf[1][:], wkv0[:], [], [("Sf", 1)], "c2a")
    DMA("sp", sh0b[:], sh0[:], [], ["sh0b"], "c2b")
    DMA("sp", convc[1][:], cv0[:], [], [("convc", 1)], "c2c")
    COPY("dve", Sb[1][0][:], Sf[1][:], [("Sf", 1)], [("Sb", 1, 0)])
    TTOP("dve", carry[1][:], sh0b[:], vecs[:, V_MU:V_MU + NSH], ALU.mult,
         ["sh0b", "vecs"], [("carry", 1)])
    for tl in (TKVall + MMall + [Tall0] + Ahat + Vhat + XVn + PPg[0] + PPg[1] + Tg[0] + Tg[1] + s_UT):
        P.op("pool", (lambda t: (lambda e: e.memset(t[:], 0.0)))(tl), [], ["pool_init"])
    P.op("pool", lambda e: e.memset(stat[:], 0.0), [], ["pool_init"])
    pool_init_tok = P.last_tok
    for i in range(3):
        P.op("dve", (lambda t: (lambda e: e.memset(t[:], 0.0)))(pm[i]), [], [("BANK", i)])
    for i in range(4):
        P.op("dve", (lambda t: (lambda e: e.memset(t[:], 0.0)))(psc[i]), [], [("BANK", 3 + i)])
    P.fences.append(pool_init_tok)
    P.fences.append(P.last_tok)

    ckpt("init")
    w_in_v = w_in.rearrange("(kc p) c -> p kc c", p=128)

    cvt_groups = []

    def cvt(g, dst_cols, src, key_extra=None):
        if not cvt_groups or cvt_groups[-1][0] != g:
            cvt_groups.append([g, []])
        rd = list(cvt_groups[-3][1]) if len(cvt_groups) >= 3 else []
        key = ("wsc", g, dst_cols[0])
        cvt_groups[-1][1].append(key)
        DMA("pool", wsc[g][:, 0:src.shape[1], dst_cols[0]:dst_cols[1]], src, rd, [key], "cv%d" % g)

    def cvt_cols(g, offs):
        for j, off in enumerate(offs):
            cvt(g, (j * 128, (j + 1) * 128), w_in_v[:, :, off:off + 128])

    def sq_view(w, g):
        return w.rearrange("(kc p) c -> p kc c", p=128)[:, :, g * 512:(g + 1) * 512]

    cvt(G_LORA, (0, 128), w_in_v[:, :, OFF_LR:OFF_LR + 128])
    for fc in range(8):
        cvt_cols(G_A + fc, [OFF_R + fc * 128, OFF_K + fc * 128, OFF_V + fc * 128, OFF_ZA + fc * 128])
    for fc in range(8):
        cvt_cols(G_B + fc, [OFF_GB + fc * 128, OFF_GC + fc * 128, OFF_XB + fc * 128, OFF_ZB + fc * 128])
    for g in range(2):
        cvt(G_WOA + g, (0, 512), sq_view(w_oa, g))
        cvt(G_GA + g, (0, 512), w_in_v[:, :, OFF_GA + g * 512:OFF_GA + (g + 1) * 512])
    for g in range(2):
        cvt(G_WOB + g, (0, 512), sq_view(w_ob, g))
        cvt(G_GB + g, (0, 512), w_in_v[:, :, OFF_GBm + g * 512:OFF_GBm + (g + 1) * 512])
    for g in range(2):
        cvt(G_WOUT + g, (0, 512), sq_view(w_out, g))
    for g in range(2):
        cvt(G_WPG + g, (0, 512), sq_view(w_pg, g))
        cvt(G_WPLE + g, (0, 512), w_ple.rearrange("(kc p) c -> p kc c", p=128)[:, :, g * 512:(g + 1) * 512])

    ckpt("cvt")
    wseq_tile = ([G_LORA] + [G_A + f for f in range(8)] + [G_B + f for f in range(8)]
                 + [G_WOA, G_GA, G_WOA + 1, G_GA + 1, G_WOB, G_GB, G_WOB + 1, G_GB + 1,
                    G_WOUT, G_WOUT + 1, G_WPG, G_WPLE, G_WPG + 1, G_WPLE + 1])
    ntiles = 1 + n_prompt_tiles
    wseq = wseq_tile * ntiles
    wstate = {"issued": 0, "used": 0}

    def wsc_keys(g):
        if G_A <= g < G_GA:
            return [("wsc", g, j * 128) for j in range(4)]
        return [("wsc", g, 0)]

    def issue_wload():
        i = wstate["issued"]
        if i >= len(wseq):
            return
        g = wseq[i]
        slot = i % NSLOT
        if g == G_LORA:
            DMA("sp", wbuf[slot][:, :, 0:128], wsc[g][:, :, 0:128], wsc_keys(g), [("wbuf", slot)], "w%d" % slot)
        elif g >= G_WPLE:
            DMA("sp", wbuf[slot][:, 0:2, :], wsc[g][:, 0:2, :], wsc_keys(g), [("wbuf", slot)], "w%d" % slot)
        else:
            DMA("sp", wbuf[slot][:], wsc[g][:], wsc_keys(g), [("wbuf", slot)], "w%d" % slot)
        wstate["issued"] = i + 1

    def next_w(expect):
        i = wstate["used"]
        assert wseq[i] == expect, (wseq[i], expect)
        while wstate["issued"] <= min(i + LOOKAHEAD, len(wseq) - 1):
            issue_wload()
        wstate["used"] = i + 1
        slot = i % NSLOT
        return wbuf[slot], ("wbuf", slot)

    def run_tile(seq, tix, x_d, p_d, y_d, tok0, TTt, C, last, phases=("head", "mid", "tail")):
        nblk = max(1, TTt // 128)
        bt = min(128, TTt)
        nch = TTt // C
        L = {64: 6, 16: 4}[C]
        xtb = xt[tix % 2]
        xk = ("xt", tix % 2)
        vm = lambda col: vecs[:, col:col + 1]
        seg_ap = segm[TTt][:, 0:TTt]
        sl = slice(0, TTt)
        m_keys = [("oa", f) for f in range(8)]

        if "head" in phases:
            DMA("sp", xtb[0:bt, 0:nblk, :], x_d[tok0:tok0 + TTt, :].rearrange("(b p) d -> p b d", p=bt),
                [("xt_store", tix % 2, tb_) for tb_ in range(4)], [xk], "x%d" % (tix % 2))
        if "mid" in phases:
            DMA("pool", pb[0:bt, 0:nblk, :], p_d[tok0:tok0 + TTt, :].rearrange("(b p) d -> p b d", p=bt),
                [], ["g_b"], "p")

        ckpt("s0a")

        NSETS = [(xnb, "xnb", 0), (t_sg[:, :].bitcast(BF16), "t_sg", 4),
                 (t_a[:, :].bitcast(BF16), "t_a", 8), (t_cs[:, :].bitcast(BF16), "t_cs", 12),
                 (t_winv[:, :].bitcast(BF16), "t_winv", 16), (t_wx[:, :].bitcast(BF16), "t_wx", 20),
                 (t_wi[:, :].bitcast(BF16), "t_wi", 24), (t_nrm[:, :].bitcast(BF16), "t_nrm", 28)]

        def par_t(*gens):
            gens = list(gens)
            while gens:
                for g_ in list(gens):
                    try:
                        next(g_)
                        yield
                    except StopIteration:
                        gens.remove(g_)

        def norm_stats(src_blk, rd_keys, si):
            jb, jk, c0 = NSETS[si]
            kss, ksd, krs = "ss%d" % si, "sd%d" % si, "rstd%d" % si
            ACT(jb[0:bt, :], src_blk, AF.Square, rd_keys, [jk])
            yield
            P.op("dve", lambda e: e.tensor_reduce(stat[0:bt, c0:c0 + 1], jb[0:bt, :], axis=AX.X, op=ALU.add),
                 [jk], [kss], dur=1.2)
            yield
            ACT(stat[0:bt, c0 + 1:c0 + 2], stat[0:bt, c0:c0 + 1], AF.Ln, [kss, "epsb0"], [ksd],
                bias=EPS_AP[0:bt, :], scale=1.0 / D)
            yield
            ACT(stat[0:bt, c0 + 2:c0 + 3], stat[0:bt, c0 + 1:c0 + 2], AF.Exp, [ksd], [krs], scale=-0.5)
            yield

        def rms_to_T(src_blk, gcol, dstT, dst_key_prefix, tb, rd_keys, coarse=False, si=0):
            jb, jk, c0 = NSETS[si]
            yield from norm_stats(src_blk, rd_keys, si)
            rstd, krs = stat[0:bt, c0 + 2:c0 + 3], "rstd%d" % si
            TS_("dve", jb[0:bt, :], src_blk, rstd, None, ALU.mult, None, rd_keys + [krs], [jk])
            yield
            TR([dict(out=ptb[:, kc * 128:kc * 128 + bt], in_=jb[0:bt, kc * 128:(kc + 1) * 128],
                     ident=ident[0:bt, 0:bt]) for kc in range(8)], [jk, "cbf"], [PTB_K])
            wkeys_ = [((dst_key_prefix, kc) if coarse else (dst_key_prefix, kc, tb)) for kc in range(8)]
            gbc = vecs[:, gcol:gcol + 8].unsqueeze(2).to_broadcast([128, 8, bt])
            TTOP("dve", dstT[:, :, tb * bt:(tb + 1) * bt],
                 ptb[:, 0:1024].rearrange("p (k t) -> p k t", k=8)[:, :, 0:bt], gbc, ALU.mult,
                 [PTB_K, "vecs"], wkeys_)
            yield

        if "head" in phases:
            yield from par_t(*[rms_to_T(xtb[0:bt, tb, :], V_GNORM, hT, "hT", tb, [xk], si=4 + tb)
                               for tb in range(nblk)])
        hT_keys = [("hT", kc, tb) for kc in range(8) for tb in range(nblk)]
        if "mid" not in phases and "tail" not in phases:
            return
        if seq == 1:
            DBG("hT", hT[:, :, 0:TTt], hT_keys, BF16)
            DBG("stat", stat[:, 0:4], ["ss", "sd", "rstd"])
            DBG("xnb", xnb[0:bt, :], ["xnb"], BF16)
        ckpt("s0d")
        for tb in (range(nblk) if "mid" in phases else ()):
            TR([dict(out=ptb[:, kc * 128:kc * 128 + bt], in_=pb[0:bt, tb, kc * 128:(kc + 1) * 128],
                     ident=ident[0:bt, 0:bt]) for kc in range(2)], ["g_b", "cbf"], [PTB_K])
            COPY("dve", pT[:, :, tb * bt:(tb + 1) * bt],
                 ptb[:, 0:256].rearrange("p (k t) -> p k t", k=2)[:, :, 0:bt], [PTB_K], [("pT", tb)])
        pT_keys = [("pT", tb) for tb in range(nblk)]

        def proj_fm(wt, wkey, col0, ncols=128, rhsT=hT, rkeys=None, nk=8):
            ps, pk = next_pm()
            MM([dict(out=ps[0:ncols, 0:TTt], lhsT=wt[:, kc, col0:col0 + ncols], rhs=rhsT[:, kc, 0:TTt],
                     start=(kc == 0), stop=(kc == nk - 1)) for kc in range(nk)],
               [wkey] + (rkeys if rkeys is not None else hT_keys), [pk])
            return ps, pk

        def shift_mix(ps, pk, idx, out_ap, out_key):
            bm = Bm[idx % 2]; bk = ("Bm", idx % 2)
            ck = ("carry", seq, idx)
            ACT(bm[:, 0:TTt], ps[:, 0:TTt], AF.Copy, [pk, "vecs"], [bk], scale=vm(V_MU + idx))
            STT("dve", out_ap[:, 1:TTt], ps[:, 1:TTt], omm[:, idx:idx + 1], bm[:, 0:TTt - 1], ALU.mult, ALU.add,
                [pk, bk, "omm"], [out_key])
            STT("dve", out_ap[:, 0:1], ps[:, 0:1], omm[:, idx:idx + 1], carry[seq][:, idx:idx + 1], ALU.mult, ALU.add,
                [pk, ck, ("carry", seq), "omm"], [out_key, (out_key, "c0")])
            COPY("pool", carry[seq][:, idx:idx + 1], bm[:, TTt - 1:TTt], [bk], [ck])
            if last:
                COPY("act", shout[seq][:, idx:idx + 1], ps[:, TTt - 1:TTt], [pk], [("shout", seq, idx)])

        if "mid" in phases:
            ckpt("s0")
            wt, wk = next_w(G_LORA)
            ps, pk = proj_fm(wt, wk, 0)
            shift_mix(ps, pk, 24, lmix, "lmix")
            ACT(lrb[0:64, 0:TTt], lmix[0:64, 0:TTt], AF.Tanh, ["lmix", ("lmix", "c0")], ["lrb0"])
            COPY("pool", lrb[64:128, 0:TTt], lmix[64:128, 0:TTt], ["lmix", ("lmix", "c0")], ["lrb1"])

            ckpt("s1")
            sl = slice(0, TTt)
            rk_ = lambda nm: [nm, (nm, "c0")]
            GS = min(4, nch)
            hrow = lambda h: slice(64 * h, 64 * h + 64)
            hC = lambda h: slice(64 * h, 64 * h + C)
            tp_ = lambda h: (64 * h, 64 * h)
            mcat = maskcat[C]

            def P_stage(fc):
                db = fc % 2
                a3 = fc % 3
                AR = ARs[a3]; t_zs = t_zss[fc % 4]; t_bon = t_bons[fc % 4]
                Bt, Kt, Vb, Bt2, Kt2 = Bts[db], Kts[db], Vbs[db], Bt2s[db], Kt2s[db]
                kBt, kKt, kVb, kBt2, kKt2 = ("Bt", db), ("Kt", db), "Vb", ("Bt2", db), ("Kt2", db)
                zk = "t_zs%d" % (fc % 4)
                wt, wk = next_w(G_A + fc)
                for j, (dst, nm) in enumerate(((t_r, "t_r"), (t_k, "t_k"), (t_v, "t_v"), (t_kh, "t_kh"))):
                    ps, pk = next_pm()
                    for half in range(2):
                        MM([dict(out=ps[:, 0:TTt], lhsT=wt[:, kc, j * 128:(j + 1) * 128], rhs=hT[:, kc, 0:TTt],
                                 start=(kc == 0), stop=(kc == 7)) for kc in range(4 * half, 4 * half + 4)],
                           [wk] + hT_keys, [pk])
                        if half == 0:
                            yield
                    shift_mix(ps, pk, (0, 8, 16, 25)[j] + fc, dst, nm)
                    yield
                ps, pk = next_pm()
                MM([dict(out=ps[:, sl], lhsT=wlora[0:64, fc * 128:(fc + 1) * 128], rhs=lrb[0:64, sl],
                         start=True, stop=True)], ["wlora", "lrb0"], [pk])
                ACT(t_sg[:, sl], ps[:, sl], AF.Sigmoid, [pk, "vecs"], ["t_sg"], bias=vm(V_WD0 + fc))
                yield
                ps, pk = next_pm()
                MM([dict(out=ps[:, sl], lhsT=wlora[64:128, fc * 128:(fc + 1) * 128], rhs=lrb[64:128, sl],
                         start=True, stop=True, tp=(64, 0))], ["wlora", "lrb1"], [pk])
                ACT(t_a[:, sl], ps[:, sl], AF.Sigmoid, [pk, "vecs"], ["t_a"], bias=vm(V_WI0 + fc))
                ACT(t_zs[:, sl], t_kh[:, sl], AF.Silu, rk_("t_kh"), [zk])
                P.op("dve", lambda e: e.tensor_tensor_scan(t_cs[:, sl], seg_ap, t_sg[:, sl], 0.0,
                                                            op0=ALU.mult, op1=ALU.add),
                     ["t_sg", "cf"], ["t_cs"], dur=0.6)
                yield
                ACT(t_wi[:, sl], t_cs[:, sl], AF.Exp, ["t_cs"], ["t_wi"], scale=-DS)
                ACT(t_winv[:, sl], t_cs[:, sl], AF.Exp, ["t_cs"], ["t_winv"], scale=DS)
                yield
                TTOP("dve", t_wx[:, sl], t_cs[:, sl], t_sg[:, sl], ALU.subtract, ["t_cs", "t_sg"], ["t_wx"])
                ACT(t_wx[:, sl], t_wx[:, sl], AF.Exp, ["t_wx"], ["t_wx"], scale=-DS)
                COPY("pool", wcb[a3][:, 0:nch], t_wi[:, sl].rearrange("p (c t) -> p c t", t=C)[:, :, C - 1],
                     ["t_wi"], [("wcb", a3)])
                yield
                ACT(t_kap[:, sl], t_k[:, sl], AF.Copy, rk_("t_k") + ["vecs"], ["t_kap"], scale=vm(V_KREM + fc))
                ACT(t_ksq[:, sl], t_kap[:, sl], AF.Square, ["t_kap"], ["t_ksq"])
                ps, pk = next_pm()
                MM([dict(out=ps[:, sl], lhsT=bones, rhs=t_ksq[:, sl], start=True, stop=True)], ["t_ksq", "cbf"], [pk])
                yield
                TS_("dve", t_nrm[:, sl], ps[:, sl], 1e-24, None, ALU.max, None, [pk], ["t_nrm"])
                ACT(t_nrm[:, sl], t_nrm[:, sl], AF.Ln, ["t_nrm"], ["t_nrm"])
                ACT(t_beta[:, sl], t_nrm[:, sl], AF.Exp, ["t_nrm"], ["t_beta"], scale=-0.5)
                yield
                TTOP("dve", t_kh[:, sl], t_kap[:, sl], t_beta[:, sl], ALU.mult, ["t_kap", "t_beta"], ["t_kh"])
                TTOP("dve", t_beta[:, sl], t_a[:, sl], t_kh[:, sl], ALU.mult, ["t_a", "t_kh"], ["t_beta"])
                ACT(t_t1[:, sl], t_a[:, sl], AF.Identity, ["t_a", "vecs", "omkr"], ["t_t1"],
                    scale=vm(V_KREP + fc), bias=omkr[:, fc:fc + 1])
                yield
                TTOP("pool", t_kt[:, sl], t_k[:, sl], t_t1[:, sl], ALU.mult, rk_("t_k") + ["t_t1"], ["t_kt"])
                TTOP("dve", AR[:, 0, sl], t_kh[:, sl], t_wx[:, sl], ALU.mult, ["t_kh", "t_wx"], [("At", a3)])
                TTOP("pool", AR[:, 1, sl], t_r[:, sl], t_wi[:, sl], ALU.mult, rk_("t_r") + ["t_wi"], [("Rt", a3)])
                yield
                TTOP("dve", Bt[:, sl], t_beta[:, sl], t_winv[:, sl], ALU.mult, ["t_beta", "t_winv"], [kBt])
                TTOP("pool", Kt[:, sl], t_kt[:, sl], t_winv[:, sl], ALU.mult, ["t_kt", "t_winv"], [kKt])
                COPY("act", Vb[:, sl], t_v[:, sl], rk_("t_v"), [kVb])
                yield
                wcbc = wcb[a3][:, 0:nch].unsqueeze(2).to_broadcast([128, nch, C])
                TTOP("dve", Bt2[:, sl].rearrange("p (c t) -> p c t", t=C), Bt[:, sl].rearrange("p (c t) -> p c t", t=C),
                     wcbc, ALU.mult, [kBt, ("wcb", a3)], [kBt2])
                TTOP("dve", Kt2[:, sl].rearrange("p (c t) -> p c t", t=C), Kt[:, sl].rearrange("p (c t) -> p c t", t=C),
                     wcbc, ALU.mult, [kKt, ("wcb", a3)], [kKt2])
                yield
                TTOP("pool", t_rk[:, sl], t_r[:, sl], t_kt[:, sl], ALU.mult, rk_("t_r") + ["t_kt"], ["t_ksq"])
                ps, pk = next_pm()
                MM([dict(out=ps[:, sl], lhsT=RB[:, fc, :], rhs=t_rk[:, sl], start=True, stop=True)],
                   ["t_ksq", ("RB", fc)], [pk])
                TTOP("dve", t_bon[:, sl], ps[:, sl], t_v[:, sl], ALU.mult, [pk] + rk_("t_v"), [("t_bon", fc % 4)])
                ACT(t_bon[:, sl], t_bon[:, sl], AF.Identity, [("t_bon", fc % 4), "vecs"], [("t_bon", fc % 4)], bias=vm(V_GNB + fc))
                yield

            def I_pre(fc, g0):
                db = fc % 2
                a3 = fc % 3
                gi = (g0 // GS) % 2
                AR = ARs[a3]
                Bt, Kt, Vb, Bt2, Kt2 = Bts[db], Kts[db], Vbs[db], Bt2s[db], Kt2s[db]
                kBt, kKt, kVb, kBt2, kKt2 = ("Bt", db), ("Kt", db), "Vb", ("Bt2", db), ("Kt2", db)
                TKV, MMa = TKVall[db], MMall[db]
                TR([dict(out=ptb[hC(h), (i * 4 + j) * 64:(i * 4 + j + 1) * 64],
                         in_=src[hrow(h), (g0 + i) * C:(g0 + i + 1) * C], ident=sident[hrow(h), 0:64], tp=tp_(h))
                    for i in range(GS) for j, src in enumerate((Bt2, Kt2, Vb, AR[:, 0, :])) for h in range(2)],
                   [kBt2, kKt2, kVb, ("At", a3), "cbf"], [PTB_K])
                COPY("act", TKV[:, g0 * 4:(g0 + GS) * 4, :],
                     ptb[:, 0:GS * 256].rearrange("p (j k) -> p j k", k=64), [PTB_K], [("TKV", db, g0)])
                yield
                for i in range(GS):
                    c = g0 + i
                    cs = slice(c * C, (c + 1) * C)
                    psM_, pkM_ = next_psc()
                    mms = []
                    for h in range(2):
                        mms.append(dict(out=psM_[hC(h), 0:2 * C], lhsT=Bt[hrow(h), cs], rhs=AR[hrow(h), :, cs],
                                        start=True, stop=True, tp=tp_(h)))
                    for h in range(2):
                        mms.append(dict(out=psM_[hC(h), 2 * C:4 * C], lhsT=Kt[hrow(h), cs], rhs=AR[hrow(h), :, cs],
                                        start=True, stop=True, tp=tp_(h)))
                    for h in range(2):
                        mms.append(dict(out=psM_[hC(h), 4 * C:5 * C], lhsT=AR[hrow(h), 0, cs], rhs=Bt[hrow(h), cs],
                                        start=True, stop=True, tp=tp_(h)))
                    MM(mms, [kBt, kKt, ("At", a3), ("Rt", a3)], [pkM_])
                    TTOP("dve", MMa[:, c, 0:5 * C], psM_[:, 0:5 * C], mcat, ALU.mult, [pkM_, "cf"], [("MM", db, c)])
                    TTOP("pool", Tg[gi][0][:, i, 0:C], MMa[:, c, 0:C], sident[:, 0:C], ALU.add, [("MM", db, c), "cbf"],
                         [("Tg", gi, 0, i)])
                    yield

            def I_inv(fc, g0):
                db = fc % 2
                gi = (g0 // GS) % 2
                MMa, Ta = MMall[db], Tall[db]
                mmk = [("MM", db, g0 + i) for i in range(GS)]
                PP = PPg[gi]; TT_ = Tg[gi]
                for j in range(1, L + 1):
                    need_sq = (j <= L - 1)
                    need_T = (j >= 2)
                    PPo = PP[(j - 1) % 2]; PPok = ("PPg", gi, (j - 1) % 2)
                    PPn = PP[j % 2]; PPnk = ("PPg", gi, j % 2)
                    To = TT_[j % 2]; Tok = [("Tg", gi, j % 2, i) for i in range(GS)]
                    def Pc_PTc(i):
                        c = g0 + i
                        if j == 1:
                            return MMa[:, c, 0:C], MMa[:, c, 4 * C:5 * C]
                        return PPo[:, i, C:2 * C], PPo[:, i, 0:C]
                    rkeys = (mmk if j == 1 else [PPok])
                    if need_sq:
                        ps_, pk_ = next_psc()
                        mms = []
                        for i in range(GS):
                            Pc, PTc = Pc_PTc(i)
                            last_sq = (j == L - 1)
                            for h in range(2):
                                mms.append(dict(out=ps_[hC(h), i * 2 * C:i * 2 * C + C], lhsT=Pc[hC(h), :],
                                                rhs=PTc[hC(h), :], start=True, stop=True, tp=tp_(h)))
                            if not last_sq:
                                for h in range(2):
                                    mms.append(dict(out=ps_[hC(h), i * 2 * C + C:(i + 1) * 2 * C], lhsT=PTc[hC(h), :],
                                                    rhs=Pc[hC(h), :], start=True, stop=True, tp=tp_(h)))
                        MM(mms, rkeys, [pk_])
                    if need_T:
                        ps3, pk3 = next_psc()
                        mms = []
                        for i in range(GS):
                            Pc, PTc = Pc_PTc(i)
                            for h in range(2):
                                mms.append(dict(out=ps3[hC(h), i * C:(i + 1) * C], lhsT=PTc[hC(h), :], rhs=To[hC(h), i, 0:C],
                                                start=True, stop=True, tp=tp_(h)))
                        MM(mms, rkeys + Tok, [pk3])
                    if need_sq:
                        COPY("act", PPn[:, 0:GS, 0:2 * C], ps_[:, 0:GS * 2 * C].rearrange("p (i k) -> p i k", k=2 * C),
                             [pk_], [PPnk])
                    if need_T:
                        if j == L:
                            dstT = Ta[:, g0:g0 + GS, 0:C]; dk = [("Tall", g0)]
                        else:
                            dstT = TT_[(j + 1) % 2][:, 0:GS, 0:C]; dk = [("Tg", gi, (j + 1) % 2, i) for i in range(GS)]
                        TTOP("dve", dstT, ps3[:, 0:GS * C].rearrange("p (i k) -> p i k", k=C), To[:, 0:GS, 0:C], ALU.add,
                             [pk3] + Tok, dk)
                    yield

            def I_post(fc, g0):
                db = fc % 2
                gi = (g0 // GS) % 2
                TKV, MMa, Ta = TKVall[db], MMall[db], Tall[db]
                tkvk = ("TKV", db, g0); tk_ = ("Tall", g0)
                mmk = [("MM", db, g0 + i) for i in range(GS)]
                ps1, pk1 = next_psc()
                MM([dict(out=ps1[hC(h), i * 64:(i + 1) * 64], lhsT=MMa[hC(h), g0 + i, 2 * C:3 * C],
                         rhs=TKV[hC(h), (g0 + i) * 4 + 2, :], start=True, stop=True, tp=tp_(h))
                    for i in range(GS) for h in range(2)], mmk + [tkvk], [pk1])
                ACT(XVn[gi][:, 0:GS, :], ps1[:, 0:GS * 64].rearrange("p (i k) -> p i k", k=64), AF.Copy,
                    [pk1], [("XVn", gi)], scale=-1.0)
                ps2, pk2 = next_psc()
                MM([dict(out=ps2[hrow(h), i * C:(i + 1) * C], lhsT=TKV[hC(h), (g0 + i) * 4 + 3, :],
                         rhs=Ta[hC(h), g0 + i, 0:C], start=True, stop=True, tp=tp_(h))
                    for i in range(GS) for h in range(2)], [tkvk, tk_], [pk2])
                ACT(Ahat[db][:, g0:g0 + GS, 0:C], ps2[:, 0:GS * C].rearrange("p (i k) -> p i k", k=C), AF.Copy,
                    [pk2], [("Ahat", db, g0)], scale=-1.0)
                yield
                ps3, pk3 = next_psc()
                MM([dict(out=ps3[hC(h), i * 64:(i + 1) * 64], lhsT=Ta[hC(h), g0 + i, 0:C],
                         rhs=XVn[gi][hC(h), i, :], start=True, stop=True, tp=tp_(h))
                    for i in range(GS) for h in range(2)], [tk_, ("XVn", gi)], [pk3])
                COPY("dve", Vhat[db][:, g0:g0 + GS, :], ps3[:, 0:GS * 64].rearrange("p (i k) -> p i k", k=64),
                     [pk3], [("Vhat", db, g0)])
                yield

            def C_stage(fc):
                db = fc % 2
                a3 = fc % 3
                AR = ARs[a3]
                TKV, MMa = TKVall[db], MMall[db]
                for c in range(nch):
                    par = c % 2
                    g0 = (c // GS) * GS
                    cs = slice(c * C, (c + 1) * C)
                    BT, KT, VT = TKV[:, c * 4, :], TKV[:, c * 4 + 1, :], TKV[:, c * 4 + 2, :]
                    NB, MK = MMa[:, c, 0:2 * C], MMa[:, c, 2 * C:4 * C]
                    tkvk, mmk_ = ("TKV", db, g0), ("MM", db, c)
                    sbi = sbidx[seq][fc]
                    Sb_old, Sb_new = Sb[seq][sbi], Sb[seq][1 - sbi]
                    So_k, Sn_k = ("Sb", seq, sbi, fc), ("Sb", seq, 1 - sbi, fc)
                    sbidx[seq][fc] = 1 - sbi
                    UT = s_UT[par]
                    psU, pkU = next_psc()
                    MM([dict(out=psU[hC(h), 0:64], lhsT=Ahat[db][hrow(h), c, 0:C], rhs=Sb_old[hrow(h), fc, :],
                             start=True, stop=True, tp=tp_(h)) for h in range(2)],
                       [("Ahat", db, g0), So_k, ("Sb", seq, sbi)], [pkU])
                    TTOP("dve", UT[:, :], psU[:, 0:64], Vhat[db][:, c, :], ALU.add, [pkU, ("Vhat", db, g0)],
                         [("UT", par)])
                    yield
                    psS, pkS = next_psc()
                    MM([d for h in range(2) for d in (
                        dict(out=psS[hrow(h), 0:64], lhsT=BT[hC(h), :], rhs=UT[hC(h), :], start=True, stop=False,
                             tp=tp_(h)),
                        dict(out=psS[hrow(h), 0:64], lhsT=KT[hC(h), :], rhs=VT[hC(h), :], start=False, stop=True,
                             tp=tp_(h)))],
                       [tkvk, ("UT", par)], [pkS])
                    wc = wcb[a3][:, c:c + 1]
                    STT("dve", Sb_new[:, fc, :], Sf[seq][:, fc, :], wc, psS[:, 0:64], ALU.mult, ALU.add,
                        [pkS, ("Sf", seq, fc), ("Sf", seq), ("wcb", a3)], [Sn_k])
                    STT("dve", Sf[seq][:, fc, :], Sf[seq][:, fc, :], wc, psS[:, 0:64], ALU.mult, ALU.add,
                        [pkS, ("Sf", seq, fc), ("Sf", seq), ("wcb", a3)], [("Sf", seq, fc)])
                    psY, pkY = next_psc()
                    MM([d for h in range(2) for d in (
                        dict(out=psY[hrow(h), 0:C], lhsT=Sb_old[hrow(h), fc, :], rhs=AR[hrow(h), 1, cs], start=True, stop=False,
                             tp=tp_(h)),
                        dict(out=psY[hrow(h), 0:C], lhsT=UT[hC(h), :], rhs=NB[hC(h), C:2 * C], start=False, stop=False,
                             tp=tp_(h)),
                        dict(out=psY[hrow(h), 0:C], lhsT=VT[hC(h), :], rhs=MK[hC(h), C:2 * C], start=False, stop=True,
                             tp=tp_(h)))],
                       [("Rt", a3), So_k, ("Sb", seq, sbi), ("UT", par), mmk_, tkvk], [pkY])
                    COPY("act", t_ys[db][:, cs], psY[:, 0:C], [pkY], [("t_y", db, c)])
                    yield

            def G_stage(fc):
                db = fc % 2
                t_zs = t_zss[fc % 4]; t_bon = t_bons[fc % 4]
                zk = "t_zs%d" % (fc % 4)
                t_y = t_ys[db]
                ykeys = [("t_y", db, c) for c in range(nch)]
                gak = ["lmix", ("lmix", "c0")]
                ACT(g_b[:, sl], t_y[:, sl], AF.Square, ykeys, ["g_b"])
                psM, pkM = next_pm()
                MM([dict(out=psM[:, sl], lhsT=bmean, rhs=t_y[:, sl], start=True, stop=True)], ykeys + ["cf"], [pkM])
                ACT(g_a[:, sl], psM[:, sl], AF.Square, [pkM], gak)
                TTOP("dve", t_y[:, sl], t_y[:, sl], psM[:, sl], ALU.subtract, ykeys + [pkM], ykeys)
                yield
                psQ, pkQ = next_pm()
                MM([dict(out=psQ[:, sl], lhsT=bmean, rhs=g_b[:, sl], start=True, stop=True)], ["g_b", "cf"], [pkQ])
                TTOP("dve", g_a[:, sl], psQ[:, sl], g_a[:, sl], ALU.subtract, [pkQ] + gak, gak)
                ACT(g_a[:, sl], g_a[:, sl], AF.Ln, gak + ["epsb1"], gak, bias=GNEPS_AP[:, :])
                ACT(g_b[:, sl], g_a[:, sl], AF.Exp, gak + ["g_b"], ["g_b"], scale=-0.5)
                yield
                TTOP("pool", t_y[:, sl], t_y[:, sl], g_b[:, sl], ALU.mult, ykeys + ["g_b"], ykeys)
                STT("dve", t_y[:, sl], t_y[:, sl], vm(V_GNW + fc), t_bon[:, sl], ALU.mult, ALU.add,
                    ykeys + [("t_bon", fc % 4), "vecs"], ykeys)
                TTOP("pool", oa[:, fc, sl], t_y[:, sl], t_zs[:, sl], ALU.mult, ykeys + [zk], [("oa", fc)])
                yield

            def drain(*gens):
                for g_ in gens:
                    for _ in g_:
                        pass

            def chain_g(*gens):
                for g_ in gens:
                    for _ in g_:
                        yield

            def rr(gens):
                gens = list(gens)
                while gens:
                    for g_ in list(gens):
                        try:
                            next(g_)
                        except StopIteration:
                            gens.remove(g_)

            def par_g(*gens):
                gens = list(gens)
                while gens:
                    for g_ in list(gens):
                        try:
                            next(g_)
                            yield
                        except StopIteration:
                            gens.remove(g_)

            def I_stage(k):
                grp = list(range(0, nch, GS))
                return par_g(*[chain_g(I_pre(k, g0), I_inv(k, g0), I_post(k, g0)) for g0 in grp])

            def rrw(gws):
                gws = [[g_, w_, 0.0] for g_, w_ in gws]
                while gws:
                    for ent in list(gws):
                        ent[2] += ent[1]
                        while ent[2] >= 1.0:
                            ent[2] -= 1.0
                            try:
                                next(ent[0])
                            except StopIteration:
                                gws.remove(ent)
                                break

            for k in range(-1, 10):
                gl = []
                if 0 <= k - 1 <= 7:
                    gl.append((C_stage(k - 1), 1.0))
                if 0 <= k <= 7:
                    gl.append((I_stage(k), 1.65))
                if 0 <= k + 1 <= 7:
                    gl.append((P_stage(k + 1), 1.25))
                if 0 <= k - 2 <= 7:
                    gl.append((G_stage(k - 2), 0.2))
                rrw(gl)

            if seq == 1:
                DBG("oa", oa[:, :, sl], [("oa", f) for f in range(8)], BF16)
            ckpt("s2")
            for fc in range(8):
                wt, wk = next_w(G_B + fc)
                sl = slice(0, TTt)
                ps_zb, k_zb = proj_fm(wt, wk, 384)
                ACT(t_zb[:, sl], ps_zb[:, sl], AF.Silu, [k_zb], ["t_cs"])
                ps_gb, k_gb = proj_fm(wt, wk, 0)
                TTOP("dve", t_g1[:, sl], ps_gb[:, sl], t_zb[:, sl], ALU.mult, [k_gb, "t_cs"], ["t_winv"])
                ps_xb, k_xb = proj_fm(wt, wk, 256)
                COPY("act", t_xbs[:, sl], ps_xb[:, sl], [k_xb], ["t_sg"])
                COPY("pool", ubuf[:, 0:2], convc[seq][:, fc, :], [("convc", seq), ("convc", seq, fc)], ["t_kap"])
                ps_gc, k_gc = proj_fm(wt, wk, 128)
                TTOP("dve", ubuf[:, 2:TTt + 2], ps_gc[:, sl], t_xbs[:, sl], ALU.mult, [k_gc, "t_sg"], ["t_kap"])
                COPY("pool", convc[seq][:, fc, :], ubuf[:, TTt:TTt + 2], ["t_kap", "t_kap"], [("convc", seq, fc)])
                ACT(t_c1[:, sl], ubuf[:, 0:TTt], AF.Copy, ["t_kap", "t_kap", "vecs"], ["t_a"], scale=vm(V_CONV + fc))
                STT("dve", t_c1[:, sl], ubuf[:, 1:TTt + 1], vm(V_CONV + 8 + fc), t_c1[:, sl], ALU.mult, ALU.add,
                    ["t_kap", "t_kap", "t_a", "vecs"], ["t_a"])
                STT("dve", t_c1[:, sl], ubuf[:, 2:TTt + 2], vm(V_CONV + 16 + fc), t_c1[:, sl], ALU.mult, ALU.add,
                    ["t_kap", "t_a", "vecs"], ["t_a"])
                TTOP("pool", ob[:, fc, sl], t_g1[:, sl], t_c1[:, sl], ALU.mult, ["t_winv", "t_a"], [("ob", fc)])

            if seq == 1:
                DBG("ob", ob[:, :, sl], [("ob", f) for f in range(8)], BF16)
            ckpt("s3")
            oa_keys = [("oa", f) for f in range(8)]
            ob_keys = [("ob", f) for f in range(8)]
            sl = slice(0, TTt)
            for g in range(2):
                woa, woak = next_w(G_WOA + g)
                wga, wgak = next_w(G_GA + g)
                for i in range(4):
                    oc = 4 * g + i
                    ps_o, k_o = proj_fm(woa, woak, i * 128, rhsT=oa, rkeys=oa_keys)
                    ps_g, k_g = proj_fm(wga, wgak, i * 128)
                    ACT(t_sga[:, sl], ps_g[:, sl], AF.Sigmoid, [k_g], ["t_wx"])
                    TTOP("dve", m1[:, oc, sl], ps_o[:, sl], t_sga[:, sl], ALU.mult, [k_o, "t_wx"], [("m1", oc)] + ALIAS_K)
            for g in range(2):
                wob, wobk = next_w(G_WOB + g)
                wgb, wgbk = next_w(G_GB + g)
                for i in range(4):
                    oc = 4 * g + i
                    ps_o, k_o = proj_fm(wob, wobk, i * 128, rhsT=ob, rkeys=ob_keys)
                    ps_g, k_g = proj_fm(wgb, wgbk, i * 128)
                    ACT(t_sga[:, sl], ps_g[:, sl], AF.Sigmoid, [k_g], ["t_wx"])
                    TTOP("dve", t_m2b[:, sl], ps_o[:, sl], t_sga[:, sl], ALU.mult, [k_o, "t_wx"], ["t_kt"])
                    TTOP("pool", oa[:, oc, sl], t_m2b[:, sl], m1[:, oc, sl], ALU.add, ["t_kt", ("m1", oc)] + ALIAS_K,
                         [("oa", oc)])
            m_keys = oa_keys

            if seq == 1:
                DBG("m", oa[:, :, sl], [("oa", f) for f in range(8)], BF16)
            ckpt("s4")
        if "tail" in phases:
            for g in range(2):
                wo, wok = next_w(G_WOUT + g)
                for tb in range(nblk):
                    ps, pk = next_pm()
                    MM([dict(out=ps[0:bt, :], lhsT=oa[:, kc, tb * bt:(tb + 1) * bt], rhs=wo[:, kc, :],
                             start=(kc == 0), stop=(kc == 7)) for kc in range(8)], [wok] + m_keys, [pk])
                    dst = xtb[0:bt, tb, g * 512:(g + 1) * 512]
                    TTOP("dve", dst, ps[0:bt, :], dst, ALU.add, [pk, xk], [("xv", tb, g)])
                    yield
            ckpt("s5")
            yield from par_t(*[rms_to_T(xtb[0:bt, tb, :], V_GPLE, ob, "ob", tb, [xk, ("xv", tb, 0), ("xv", tb, 1)],
                                        coarse=True, si=tb) for tb in range(nblk)])
            h2T_keys = [("ob", kc) for kc in range(8)]
            for g in range(2):
                wg_, wgk_ = next_w(G_WPG + g)
                wp_, wpk_ = next_w(G_WPLE + g)
                for tb in range(nblk):
                    psg, pkg = next_pm()
                    MM([dict(out=psg[0:bt, :], lhsT=ob[:, kc, tb * bt:(tb + 1) * bt], rhs=wg_[:, kc, :],
                             start=(kc == 0), stop=(kc == 7)) for kc in range(8)], [wgk_] + h2T_keys, [pkg])
                    psp, pkp = next_pm()
                    MM([dict(out=psp[0:bt, :], lhsT=pT[:, kc, tb * bt:(tb + 1) * bt], rhs=wp_[:, kc, :],
                             start=(kc == 0), stop=(kc == 1)) for kc in range(2)], [wpk_] + pT_keys, [pkp])
                    tf_, tfk_ = ((t_t1, "t_t1"), (t_kt, "t_kt"))[tb % 2]
                    ACT(tf_[0:bt, :], psg[0:bt, :], AF.Sigmoid, [pkg], [tfk_])
                    TTOP("dve", tf_[0:bt, :], tf_[0:bt, :], psp[0:bt, :], ALU.mult, [tfk_, pkp], [tfk_])
                    dst = xtb[0:bt, tb, g * 512:(g + 1) * 512]
                    TTOP("pool", dst, dst, tf_[0:bt, :], ALU.add, [tfk_, xk, ("xv", tb, g)],
                         [("xv", tb, g)])
                    yield
            ckpt("s6")
            def fin_tb(tb):
                src = xtb[0:bt, tb, :]
                rd = [xk, ("xv", tb, 0), ("xv", tb, 1)]
                yield from norm_stats(src, rd, tb)
                c0_ = NSETS[tb][2]
                STT("dve", src, src, stat[0:bt, c0_ + 2:c0_ + 3], gfin[0:bt, :], ALU.mult, ALU.mult,
                    rd + ["rstd%d" % tb, "gfin"], [("xv", tb, 0), ("xv", tb, 1)])
                yield
                DMA("act", y_d[tok0 + tb * bt:tok0 + (tb + 1) * bt, :], src, [("xv", tb, 0), ("xv", tb, 1)],
                    [("y_out", seq, tix, tb), ("xt_store", tix % 2, tb)], "yo%d_%d" % (tix % 2, tb))
                yield

            yield from par_t(*[fin_tb(tb) for tb in range(nblk)])
            if last:
                fk = [("Sf", seq, f) for f in range(8)]
                DMA("act", o_wkv[seq][:], Sf[seq][:], fk + [("Sf", seq)], [("o_wkv", seq)], "so")
                DMA("act", o_sh[seq][:], shout[seq][:], [("shout", seq, i) for i in range(NSH)],
                    [("o_sh", seq)], "so")
                DMA("act", o_cv[seq][:], convc[seq][:], [("convc", seq, f) for f in range(8)] + [("convc", seq)],
                    [("o_cv", seq)], "so")

    sbidx = [[0] * 8, [0] * 8]

    def xk_dummy(tix):
        return ("xt_store", tix % 2)

    def rr_top(gens):
        gens = list(gens)
        while gens:
            for g_ in list(gens):
                try:
                    next(g_)
                except StopIteration:
                    gens.remove(g_)

    spos = SAMPLE_POS_DEFAULT
    spos = max(0, min(spos, n_prompt_tiles))
    tiles = [("p", t) for t in range(n_prompt_tiles)]
    tiles.insert(spos, ("s", 0))

    def mk(pos, phases):
        kind, t = tiles[pos]
        if kind == "p":
            return run_tile(0, pos, xP, pP, yP, t * TT_P, TT_P, 64, t == n_prompt_tiles - 1, phases)
        return run_tile(1, pos, xS, pS, yS, 0, TS, 16, True, phases)

    rr_top([mk(0, ("head",))])
    for pos in range(len(tiles)):
        rr_top([mk(pos, ("mid",))])
        gl = [mk(pos, ("tail",))]
        if pos + 1 < len(tiles):
            gl.append(mk(pos + 1, ("head",)))
        rr_top(gl)

    P.schedule()
    sems = {}
    for k in P.sem_keys():
        sems[k] = es.enter_context(nc.semaphore(k.replace(":", "_")))
    block = es.enter_context(nc.Block())

    def replay(e, eng, final=False):
        for waits, fn, inc in P.q[eng]:
            for sk, v in waits:
                e.wait_ge(sems[sk], v)
            ins = fn(e)
            ins.then_inc(sems[inc[0]], inc[1])
        if final:
            for sk, v in P.dcnt.items():
                e.wait_ge(sems[sk], v)

    @block.tensor
    def _(e):
        replay(e, "pe")

    @block.scalar
    def _(e):
        replay(e, "act", final=True)

    @block.vector
    def _(e):
        replay(e, "dve")

    @block.gpsimd
    def _(e):
        replay(e, "pool")

    @block.sync
    def _(e):
        replay(e, "sp")

    es.close()
    return nc


_CACHE = {}


def _consts():
    cbf = np.zeros((128, 320), np.float32)
    cbf[:, 0:128] = np.eye(128)
    for h in range(2):
        cbf[64 * h:64 * h + 64, 128:192] = np.eye(64)
        cbf[64 * h:64 * h + 64, 192 + 64 * h:192 + 64 * h + 64] = 1.0
    W = 128 + 64 + 5 * 128 + 5 * 32 + TT_P + TS
    cf = np.zeros((128, W), np.float32)
    cf[:, 0:128] = cbf[:, 192:320] / 64.0
    cf[:, 128:192] = cbf[:, 128:192]
    o = 192
    for C, base, w in ((64, o, 128), (16, o + 640, 32)):
        s = (np.arange(128) % 64)[:, None]
        t = np.arange(C)[None, :]
        msu = (s < t).astype(np.float32)
        miu = (s <= t).astype(np.float32)
        msl = (t < s).astype(np.float32)
        cf[:, base:base + w] = np.concatenate([-msu, miu], 1)
        cf[:, base + w:base + 2 * w] = np.concatenate([msu, miu], 1)
        cf[:, base + 2 * w:base + 2 * w + C] = -msl
    o2 = o + 640 + 160
    seg = np.ones(TT_P, np.float32); seg[::64] = 0.0
    cf[:, o2:o2 + TT_P] = seg[None, :]
    seg2 = np.ones(TS, np.float32); seg2[0] = 0.0
    cf[:, o2 + TT_P:o2 + TT_P + TS] = seg2[None, :]
    return cbf.astype(ml_dtypes.bfloat16), cf


def _fm(v):
    v = np.asarray(v, np.float32).reshape(-1)
    return np.ascontiguousarray(v.reshape(-1, 128).T)


def kernel(x_prompt, x_sample, state_wkv, state_shift, state_conv, p_prompt, p_sample,
           g_norm, w_in, mu_shift, w_decay0, w_decay2, w_iclr0, w_iclr2, k_removal,
           k_replace, r_bonus, gn_w, gn_b, conv_w, w_o_a, w_o_b, w_out, g_ple,
           w_ple_gate, w_ple, g_final, _n_prompt_tiles=TP // TT_P, _stop=None, _debug=False):
    f = lambda a: np.ascontiguousarray(np.asarray(a, np.float32))
    nt = _n_prompt_tiles
    if (nt, _stop) not in _CACHE:
        _CACHE[(nt, _stop)] = build_program(nt, _stop, debug=_debug)
    nc = _CACHE[(nt, _stop)]
    cbf, cf = _consts()
    vecs = np.zeros((128, NV), np.float32)
    vecs[:, V_GNORM:V_GNORM + 8] = _fm(g_norm[0])
    vecs[:, V_MU:V_MU + NSH] = _fm(mu_shift[0])
    vecs[:, V_WD0:V_WD0 + 8] = _fm(w_decay0[0])
    vecs[:, V_WI0:V_WI0 + 8] = _fm(w_iclr0[0])
    vecs[:, V_KREM:V_KREM + 8] = _fm(k_removal[0])
    vecs[:, V_KREP:V_KREP + 8] = _fm(k_replace[0])
    vecs[:, V_GNW:V_GNW + 8] = _fm(gn_w[0])
    vecs[:, V_GNB:V_GNB + 8] = _fm(gn_b[0])
    for j in range(3):
        vecs[:, V_CONV + 8 * j:V_CONV + 8 * j + 8] = _fm(conv_w[0, j])
    vecs[:, V_GPLE:V_GPLE + 8] = _fm(g_ple[0])
    vecs[:, V_RBON:V_RBON + 8] = _fm(r_bonus[0])
    gfin_bc = np.ascontiguousarray(np.broadcast_to(f(g_final)[None, :], (128, D)))
    w_lora = np.ascontiguousarray(np.concatenate([f(w_decay2[0]), f(w_iclr2[0])], 0))
    shared = {
        "w_in": f(w_in[0]), "w_o_a": f(w_o_a[0]), "w_o_b": f(w_o_b[0]), "w_out": f(w_out[0]),
        "w_ple_gate": f(w_ple_gate[0]), "w_ple": f(w_ple[0]), "w_lora": w_lora,
        "vecs": vecs, "gfin_bc": gfin_bc, "cbf": cbf, "cf32": cf,
    }
    in_maps = []
    for c in range(8):
        wkv = f(state_wkv[0, c])
        wkvT = wkv.reshape(8, 2, 64, 64).transpose(1, 3, 0, 2).reshape(128, 8, 64)
        sh = _fm(state_shift[0, c])
        cv = f(state_conv[0, c])
        cvf = np.ascontiguousarray(cv.reshape(2, 8, 128).transpose(2, 1, 0))
        m = dict(shared)
        m.update({
            "x_p": f(x_prompt[c]), "x_s": f(x_sample[c]), "p_p": f(p_prompt[0, c]), "p_s": f(p_sample[0, c]),
            "wkv0T": np.ascontiguousarray(wkvT), "shift0": sh, "conv0": cvf,
        })
        in_maps.append(m)
    res = run_bass_kernel_spmd(nc, in_maps, core_ids=list(range(8)))
    R = res.results
    if _debug:
        global DBG_OUT
        DBG_OUT = [{n: np.asarray(R[c][n]) for n in DBG_LIST} for c in range(8)]

    def un_wkv(a):
        return np.ascontiguousarray(a.reshape(2, 64, 8, 64).transpose(2, 0, 3, 1).reshape(16, 64, 64))

    def un_fm(a):
        return np.ascontiguousarray(a.T.reshape(-1))

    def un_cv(a):
        return np.ascontiguousarray(a.transpose(2, 1, 0).reshape(2, 1024))

    y_p = np.stack([R[c]["y_p"] for c in range(8)]).astype(np.float32)
    y_s = np.stack([R[c]["y_s"] for c in range(8)]).astype(np.float32)
    wkv_p = np.stack([un_wkv(R[c]["wkv_p"]) for c in range(8)])[None].astype(np.float32)
    sh_p = np.stack([un_fm(R[c]["shift_p"]) for c in range(8)])[None].astype(np.float32)
    cv_p = np.stack([un_cv(R[c]["conv_p"]) for c in range(8)])[None].astype(np.float32)
    wkv_s = np.stack([un_wkv(R[c]["wkv_s"]) for c in range(8)])[None].astype(np.float32)
    sh_s = np.stack([un_fm(R[c]["shift_s"]) for c in range(8)])[None].astype(np.float32)
    cv_s = np.stack([un_cv(R[c]["conv_s"]) for c in range(8)])[None].astype(np.float32)
    return (y_p, y_s, wkv_p, sh_p, cv_p, wkv_s, sh_s, cv_s)
```
